# Optimizing a Trainium2 kernel written in Bass

```python
import math
import jax, jax.numpy as jnp
from jax import lax
import numpy as np

D_MODEL = 1024
BATCH = 8
SEQ = 2048
DEPTH = 1
DEC_BATCH = 8
DEC_SEQ = 8192
PAST_LEN = 128

HG_HEADS = 4
HG_HEAD_DIM = 128
HG_WIDTH = HG_HEADS * HG_HEAD_DIM
HG_CHUNK = 64
HG_SCALE = HG_HEAD_DIM ** -0.5
DA_HEADS = 4
DA_HEAD_DIM = 64
DA_V_DIM = 2 * DA_HEAD_DIM
DA_QK_WIDTH = DA_HEADS * 2 * DA_HEAD_DIM
DA_V_WIDTH = DA_HEADS * DA_V_DIM
DA_SCALE = DA_HEAD_DIM ** -0.5
Q_BLOCK = 128
ROT_DIM = DA_HEAD_DIM // 4
ROPE_THETA = 500000.0
D_FF = 4 * D_MODEL
NORM_EPS = 1e-6
SUBLN_EPS = 1e-5
IN_SIZES = (DA_QK_WIDTH, DA_QK_WIDTH, DA_V_WIDTH,
            HG_WIDTH, HG_WIDTH, HG_WIDTH, HG_WIDTH, HG_WIDTH,
            D_MODEL, D_MODEL)
IN_WIDTH = 3 * 512 + 5 * HG_WIDTH + 2 * D_MODEL

kernel_name = 'hgrn2_diffattn_parallel_encoder'

F32 = jnp.float32


def _rmsnorm(x, g, eps=NORM_EPS):
    xf = x.astype(F32)
    y = xf * lax.rsqrt(jnp.mean(xf * xf, axis=-1, keepdims=True) + eps) * g.astype(F32)
    return y.astype(x.dtype)


def _gla_chunkwise(q, k, v, log_f):
    B, H, T, dk = q.shape
    dv = v.shape[-1]
    n = T // HG_CHUNK

    def chunks(a):
        return a.reshape(B, H, n, HG_CHUNK, a.shape[-1]).transpose(2, 0, 1, 3, 4)

    qc, kc, vc = chunks(q), chunks(k), chunks(v)
    gc = jnp.cumsum(chunks(log_f), axis=3)
    mask = jnp.tril(jnp.ones((HG_CHUNK, HG_CHUNK), dtype=bool))[:, :, None]

    def step(S, inp):
        q_, k_, v_, g_ = inp
        diff = g_[:, :, :, None, :] - g_[:, :, None, :, :]
        decay = jnp.exp(jnp.where(mask, diff, -jnp.inf))
        a = jnp.einsum('bhtk,bhsk,bhtsk->bhts', q_, k_, decay)
        o = (jnp.einsum('bhts,bhsv->bhtv', a, v_)
             + jnp.einsum('bhtk,bhkv->bhtv', q_ * jnp.exp(g_), S))
        g_last = g_[:, :, -1:, :]
        S = (S * jnp.exp(g_last[:, :, 0, :])[..., None]
             + jnp.einsum('bhsk,bhsv->bhkv', k_ * jnp.exp(g_last - g_), v_))
        return S, o

    S0 = jnp.zeros((B, H, dk, dv), F32)
    _, o = lax.scan(step, S0, (qc, kc, vc, gc))
    return o.transpose(1, 2, 0, 3, 4).reshape(B, H, T, dv)


def _partial_rotary(x, pos):
    half = ROT_DIM // 2
    inv_freq = ROPE_THETA ** (-jnp.arange(0, ROT_DIM, 2, dtype=F32) / ROT_DIM)
    ang = pos[:, None] * inv_freq[None, :]
    cos, sin = jnp.cos(ang), jnp.sin(ang)
    x1, x2, xp = x[..., :half], x[..., half:ROT_DIM], x[..., ROT_DIM:]
    return jnp.concatenate([x1 * cos - x2 * sin, x2 * cos + x1 * sin, xp], axis=-1)


def _diff_attention(q, k, v, lam):
    B, H2, T, d = q.shape
    H = H2 // 2
    nb = T // Q_BLOCK
    qb = q.reshape(B, H2, nb, Q_BLOCK, d).transpose(2, 0, 1, 3, 4)

    def block(qi):
        s = jnp.einsum('bhqd,bhkd->bhqk', qi, k) * DA_SCALE
        p = jax.nn.softmax(s, axis=-1).reshape(B, H, 2, Q_BLOCK, T)
        w = p[:, :, 0] - lam * p[:, :, 1]
        return jnp.einsum('bhqk,bhkv->bhqv', w, v)

    o = lax.map(block, qb)
    return o.transpose(1, 2, 0, 3, 4).reshape(B, H, T, v.shape[-1])


def _layer(x, pos, lidx, norm1, w_in, hg_lb_logits, hg_norm, w_hg_branch,
           lq1, lk1, lq2, lk2, da_subln, w_da_branch, w_out, norm2, w_mlp_in, w_mlp_out):
    B, T, _ = x.shape
    h = _rmsnorm(x, norm1)
    u = h @ w_in
    offs = np.cumsum(np.array(IN_SIZES))[:-1].tolist()
    q_da, k_da, v_da, q_hg, f_fw, f_bw, i_hg, g_hg, gate_hg, gate_da = jnp.split(u, offs, axis=-1)

    lb = jnp.cumsum(jax.nn.softmax(hg_lb_logits.astype(F32), axis=1), axis=1)[:, lidx]
    ff = lb[0] + (1.0 - lb[0]) * jax.nn.sigmoid(f_fw.astype(F32))
    fb = lb[1] + (1.0 - lb[1]) * jax.nn.sigmoid(f_bw.astype(F32))

    def heads(a):
        return a.reshape(B, T, HG_HEADS, HG_HEAD_DIM).transpose(0, 2, 1, 3)

    def rev(a):
        return jnp.flip(a, axis=2)

    q = heads(jax.nn.silu(q_hg.astype(F32)) * HG_SCALE)
    i = heads(i_hg.astype(F32))
    ff, fb = heads(ff), heads(fb)
    o = _gla_chunkwise(jnp.concatenate([q, rev(q)], axis=1),
                       jnp.concatenate([1.0 - ff, rev(1.0 - fb)], axis=1),
                       jnp.concatenate([i, rev(i)], axis=1),
                       jnp.concatenate([jnp.log(ff), rev(jnp.log(fb))], axis=1))
    o = o[:, :HG_HEADS] + rev(o[:, HG_HEADS:])
    o = _rmsnorm(o.transpose(0, 2, 1, 3), hg_norm.reshape(HG_HEADS, HG_HEAD_DIM))
    o = o.reshape(B, T, HG_WIDTH) * jax.nn.silu(g_hg.astype(F32))
    y_hg = o.astype(x.dtype) @ w_hg_branch

    lam_init = 0.8 - 0.6 * math.exp(-0.3 * lidx)
    lam = (jnp.exp(jnp.sum(lq1.astype(F32) * lk1.astype(F32)))
           - jnp.exp(jnp.sum(lq2.astype(F32) * lk2.astype(F32))) + lam_init)
    qa = q_da.astype(F32).reshape(B, T, 2 * DA_HEADS, DA_HEAD_DIM).transpose(0, 2, 1, 3)
    ka = k_da.astype(F32).reshape(B, T, 2 * DA_HEADS, DA_HEAD_DIM).transpose(0, 2, 1, 3)
    va = v_da.astype(F32).reshape(B, T, DA_HEADS, DA_V_DIM).transpose(0, 2, 1, 3)
    oa = _diff_attention(_partial_rotary(qa, pos), _partial_rotary(ka, pos), va, lam)
    oa = _rmsnorm(oa, da_subln, SUBLN_EPS) * (1.0 - lam_init)
    oa = oa.transpose(0, 2, 1, 3).reshape(B, T, DA_V_WIDTH)
    y_da = oa.astype(x.dtype) @ w_da_branch

    m = (jax.nn.sigmoid(gate_hg.astype(F32)) * y_hg.astype(F32)
         + jax.nn.sigmoid(gate_da.astype(F32)) * y_da.astype(F32))
    x = x + m.astype(x.dtype) @ w_out

    h2 = _rmsnorm(x, norm2)
    x = x + jnp.square(jax.nn.relu(h2 @ w_mlp_in)) @ w_mlp_out
    return x


def _trunk(x, norm1, w_in, hg_lb_logits, hg_norm, w_hg_branch, da_lambda_q1, da_lambda_k1,
           da_lambda_q2, da_lambda_k2, da_subln, w_da_branch, w_out, norm2, w_mlp_in,
           w_mlp_out, final_norm):
    pos = jnp.arange(x.shape[1], dtype=F32)
    for l in range(DEPTH):
        x = _layer(x, pos, l, norm1[l], w_in[l], hg_lb_logits, hg_norm[l], w_hg_branch[l],
                   da_lambda_q1[l], da_lambda_k1[l], da_lambda_q2[l], da_lambda_k2[l],
                   da_subln[l], w_da_branch[l], w_out[l], norm2[l], w_mlp_in[l], w_mlp_out[l])
    return _rmsnorm(x, final_norm)


def setup_inputs(seed: int = 0) -> dict:
    key = jax.random.key(seed)
    ks = jax.random.split(key, 18)

    def nrm(k, shape, scale):
        return jax.random.normal(k, shape, F32) * scale

    return {
        'x_prompt': nrm(ks[0], (BATCH, SEQ, D_MODEL), 1.0),
        'x_sample': nrm(ks[1], (DEC_BATCH, DEC_SEQ, D_MODEL), 1.0),
        'norm1': 1.0 + nrm(ks[2], (DEPTH, D_MODEL), 0.02),
        'w_in': nrm(ks[3], (DEPTH, D_MODEL, IN_WIDTH), D_MODEL ** -0.5),
        'hg_lb_logits': nrm(ks[4], (2, DEPTH + 1, HG_WIDTH), 1.0),
        'hg_norm': 1.0 + nrm(ks[5], (DEPTH, HG_WIDTH), 0.02),
        'w_hg_branch': nrm(ks[6], (DEPTH, HG_WIDTH, D_MODEL), HG_WIDTH ** -0.5),
        'da_lambda_q1': nrm(ks[7], (DEPTH, DA_HEAD_DIM), 0.1),
        'da_lambda_k1': nrm(ks[8], (DEPTH, DA_HEAD_DIM), 0.1),
        'da_lambda_q2': nrm(ks[9], (DEPTH, DA_HEAD_DIM), 0.1),
        'da_lambda_k2': nrm(ks[10], (DEPTH, DA_HEAD_DIM), 0.1),
        'da_subln': 1.0 + nrm(ks[11], (DEPTH, DA_V_DIM), 0.02),
        'w_da_branch': nrm(ks[12], (DEPTH, DA_V_WIDTH, D_MODEL), DA_V_WIDTH ** -0.5),
        'w_out': nrm(ks[13], (DEPTH, D_MODEL, D_MODEL), D_MODEL ** -0.5),
        'norm2': 1.0 + nrm(ks[14], (DEPTH, D_MODEL), 0.02),
        'w_mlp_in': nrm(ks[15], (DEPTH, D_MODEL, D_FF), D_MODEL ** -0.5),
        'w_mlp_out': nrm(ks[16], (DEPTH, D_FF, D_MODEL), D_FF ** -0.5),
        'final_norm': 1.0 + nrm(ks[17], (D_MODEL,), 0.02),
    }


def reference(x_prompt, x_sample, norm1, w_in, hg_lb_logits, hg_norm, w_hg_branch,
              da_lambda_q1, da_lambda_k1, da_lambda_q2, da_lambda_k2, da_subln, w_da_branch,
              w_out, norm2, w_mlp_in, w_mlp_out, final_norm):
    y_prompt = _trunk(x_prompt, norm1, w_in, hg_lb_logits, hg_norm, w_hg_branch, da_lambda_q1,
                      da_lambda_k1, da_lambda_q2, da_lambda_k2, da_subln, w_da_branch, w_out,
                      norm2, w_mlp_in, w_mlp_out, final_norm)
    y_sample = _trunk(x_sample, norm1, w_in, hg_lb_logits, hg_norm, w_hg_branch, da_lambda_q1,
                      da_lambda_k1, da_lambda_q2, da_lambda_k2, da_subln, w_da_branch, w_out,
                      norm2, w_mlp_in, w_mlp_out, final_norm)
    return (y_prompt, y_sample)
```

```python
import math
from contextlib import ExitStack

import numpy as np
import ml_dtypes
import concourse.bass as bass
import concourse.mybir as mybir
from concourse.bass_utils import run_bass_kernel_spmd

F32 = mybir.dt.float32
BF16 = mybir.dt.bfloat16
U8 = mybir.dt.uint8
AF = mybir.ActivationFunctionType
ALU = mybir.AluOpType

D = 1024
INW = 6144
DFF = 4096
HG_SCALE = 128 ** -0.5
DA_SCALE = 64 ** -0.5
NORM_EPS = 1e-6
SUBLN_EPS = 1e-5
LAM_INIT = 0.8 - 0.6 * math.exp(-0.3 * 0)
ROPE_THETA = 500000.0
NCORES = 8

COMPUTE = ("pe", "act", "dve", "pool")


class _Op:
    __slots__ = ("eng", "emit", "dma", "key", "pos", "waits", "signal", "sigval", "vcdone", "cum")


class Sched:
    def __init__(self):
        self.ops = []
        self.last_w = {}
        self.readers = {}
        self.npos = {e: 0 for e in COMPUTE + ("sp",)}
        self.keycount = {}
        self.extra = []
        self.rawdeps = []

    def add(self, eng, emit, R=(), W=(), key=None):
        op = _Op()
        op.eng = eng
        op.emit = emit
        op.dma = key is not None
        op.key = key
        op.pos = self.npos[eng]
        self.npos[eng] += 1
        op.waits = []
        op.signal = False
        op.sigval = None
        j = len(self.ops)
        deps = {}
        for i in self.extra:
            deps[i] = "BAR"
        for r in R:
            i = self.last_w.get(r)
            if i is not None:
                deps[i] = "RAW"
        for r in W:
            i = self.last_w.get(r)
            if i is not None and deps.get(i) != "RAW":
                deps.setdefault(i, "WAW")
            for i in self.readers.get(r, {}).values():
                if i != j:
                    deps.setdefault(i, "WAR")
        for r in W:
            self.last_w[r] = j
            self.readers[r] = {}
        for r in R:
            if r in W:
                continue
            k = ("k", key) if op.dma else eng
            self.readers.setdefault(r, {})[k] = j
        if op.dma:
            self.keycount[key] = self.keycount.get(key, 0) + 1
            op.cum = self.keycount[key]
        self.ops.append(op)
        self.rawdeps.append((deps, dict(self.keycount)))
        return j

    def barrier(self):
        last = {}
        for j, op in enumerate(self.ops):
            if op.dma:
                last[("k", op.key)] = j
            else:
                last[op.eng] = j
        self.extra = list(last.values())

    def resolve(self):
        vc = {e: {} for e in self.npos}
        for j, op in enumerate(self.ops):
            deps, kc = self.rawdeps[j]
            C = op.eng
            cur = vc[C]
            for i in sorted(deps):
                typ = deps[i]
                p = self.ops[i]
                if p.dma:
                    kk = "k:" + p.key
                    if op.dma and p.key == op.key:
                        if typ == "WAW":
                            continue
                        val = op.cum - 1
                    elif typ == "BAR":
                        val = p.cum
                    else:
                        val = kc.get(p.key, p.cum)
                    if cur.get(kk, 0) >= val:
                        continue
                    op.waits.append(("k", p.key, val))
                    cur[kk] = val
                    for a, b in p.vcdone.items():
                        if cur.get(a, -1) < b:
                            cur[a] = b
                else:
                    E = p.eng
                    if E == C and not op.dma and E == "pe":
                        continue
                    if cur.get(E, -1) >= p.pos:
                        continue
                    op.waits.append(("e", i))
                    p.signal = True
                    for a, b in p.vcdone.items():
                        if cur.get(a, -1) < b:
                            cur[a] = b
                    cur[E] = p.pos
            op.vcdone = dict(cur)
            if not op.dma:
                op.vcdone[C] = op.pos
        cnt = {e: 0 for e in COMPUTE}
        for op in self.ops:
            if not op.dma and op.signal:
                cnt[op.eng] += 1
                op.sigval = cnt[op.eng]

    def emit(self, nc, es):
        self.resolve()
        sems = {e: es.enter_context(nc.semaphore("s_" + e)) for e in COMPUTE}
        ksems = {k: es.enter_context(nc.semaphore("k_" + k)) for k in self.keycount}
        ops = self.ops

        def stream(engname, e):
            for op in ops:
                if op.eng != engname:
                    continue
                for w in op.waits:
                    if w[0] == "k":
                        e.wait_ge(ksems[w[1]], 16 * w[2])
                    else:
                        p = ops[w[1]]
                        e.wait_ge(sems[p.eng], p.sigval)
                ins = op.emit(e)
                if op.dma:
                    ins.then_inc(ksems[op.key], 16)
                elif op.signal:
                    ins.then_inc(sems[op.eng], 1)
            if engname == "sp":
                for k, c in self.keycount.items():
                    e.wait_ge(ksems[k], 16 * c)

        with nc.Block() as block:
            @block.sync
            def _(e):
                stream("sp", e)

            @block.tensor
            def _(e):
                stream("pe", e)

            @block.scalar
            def _(e):
                stream("act", e)

            @block.vector
            def _(e):
                stream("dve", e)

            @block.gpsimd
            def _(e):
                stream("pool", e)


class Builder:
    def __init__(self, nc, es, TS, debug=False):
        self.nc = nc
        self.es = es
        self.TS = TS
        self.S = Sched()
        self.debug = debug
        self.big = es.enter_context(nc.sbuf_tensor("big", [128, 206 * 1024], U8))
        self.psall = es.enter_context(nc.psum_tensor("psall", [128, 4096], F32))
        self.ps = [self.psall[:, i * 512:(i + 1) * 512] for i in range(8)]
        self.off = 0
        self.uid = 0

    def reset(self):
        self.off = 0

    def alloc(self, shape, dt):
        esz = 4 if dt == F32 else 2
        n = int(np.prod(shape[1:]))
        nb = (n * esz + 31) // 32 * 32
        assert self.off + nb <= 206 * 1024, ("SBUF overflow", self.off, nb)
        v = self.big[0:shape[0], self.off:self.off + n * esz].bitcast(dt)
        self.off += nb
        if len(shape) == 3:
            v = v.rearrange("p (a b) -> p a b", b=shape[2])
        elif len(shape) == 4:
            v = v.rearrange("p (a b c) -> p a b c", b=shape[2], c=shape[3])
        return v

    def name(self, s):
        self.uid += 1
        return "%s_%d" % (s, self.uid)

    def psb(self, i):
        return self.ps[i].bitcast(BF16)

    def dma(self, out, in_, key, R=(), W=(), **kw):
        self.S.add("sp", lambda e: e.dma_start(out=out, in_=in_, **kw), R, W, key=key)

    def mm(self, out, lhsT, rhs, start=True, stop=True, R=(), W=()):
        self.S.add("pe", lambda e: e.matmul(out, lhsT=lhsT, rhs=rhs, start=start, stop=stop), R, W)

    def tr(self, out, in_, ident, R=(), W=()):
        self.S.add("pe", lambda e: e.transpose(out, in_, ident), R, W)

    def act(self, out, in_, func, R=(), W=(), **kw):
        self.S.add("act", lambda e: e.activation(out=out, in_=in_, func=func, **kw), R, W)

    def dve(self, fn, R=(), W=()):
        self.S.add("dve", fn, R, W)

    def pool(self, fn, R=(), W=()):
        self.S.add("pool", fn, R, W)


def _tt(out, a, b, op):
    return lambda e: e.tensor_tensor(out=out, in0=a, in1=b, op=op)


def _ts(out, a, s1, s2, op0, op1=None):
    if op1 is None:
        return lambda e: e.tensor_scalar(out=out, in0=a, scalar1=s1, scalar2=None, op0=op0)
    return lambda e: e.tensor_scalar(out=out, in0=a, scalar1=s1, scalar2=s2, op0=op0, op1=op1)


def _stt(out, a, s, b, op0, op1):
    return lambda e: e.scalar_tensor_tensor(out=out, in0=a, scalar=s, in1=b, op0=op0, op1=op1)


def _cp(out, a):
    return lambda e: e.tensor_copy(out=out, in_=a)


def build_program(nc, TS, debug=False):
    es = ExitStack()
    B = Builder(nc, es, TS, debug)
    S = B.S
    nseq = len(TS)
    dk = "ExternalOutput" if debug else "Internal"

    def din(name, shape, dt=F32):
        return nc.dram_tensor(name, list(shape), dt, kind="ExternalInput").ap()

    xs = [din("x%d" % i, [T, D]) for i, T in enumerate(TS)]
    ys = [nc.dram_tensor("y%d" % i, [T, D], F32, kind="ExternalOutput").ap() for i, T in enumerate(TS)]
    norm1 = din("norm1", [D])
    w_in = din("w_in", [D, INW])
    lbl = din("hg_lb_logits", [2, 2, 512])
    hg_norm = din("hg_norm", [512])
    w_hg = din("w_hg_branch", [512, D])
    lq1 = din("lq1", [64]); lk1 = din("lk1", [64]); lq2 = din("lq2", [64]); lk2 = din("lk2", [64])
    subln = din("da_subln", [128])
    w_da = din("w_da_branch", [512, D])
    w_out = din("w_out", [D, D])
    norm2 = din("norm2", [D])
    w1 = din("w_mlp_in", [D, DFF])
    w2 = din("w_mlp_out", [DFF, D])
    fnorm = din("final_norm", [D])
    c_ident = din("c_ident", [128, 128], BF16)
    c_rt = din("c_rt", [128, 128], BF16)
    c_ones = din("c_ones", [128, 128], BF16)
    c_inv32 = din("c_inv32", [64, 128])
    TMAX = max(TS)
    c_cos = din("c_cos", [128, TMAX])
    c_sin = din("c_sin", [128, TMAX])
    c_gmask = din("c_gmask", [2, 64, 256])
    c_reset = din("c_reset", [128, 2, 512])

    def dscr(name, shape, dt):
        return nc.dram_tensor(name, list(shape), dt, kind=dk).ap()

    sc = []
    for i, T in enumerate(TS):
        d = {}
        d["qT"] = dscr("s%d_qT" % i, [4, 128, T], BF16)
        d["kT"] = dscr("s%d_kT" % i, [4, 128, T], BF16)
        d["vA"] = dscr("s%d_vA" % i, [4, 128, T // 128, 128], BF16)
        d["qg"] = dscr("s%d_qg" % i, [2, 4, 128, T], BF16)
        d["kg"] = dscr("s%d_kg" % i, [2, 4, 128, T], BF16)
        d["kt"] = dscr("s%d_kt" % i, [2, T, 512], BF16)
        d["vg"] = dscr("s%d_vg" % i, [T, 512], BF16)
        d["dd"] = dscr("s%d_dd" % i, [2, 128, 4, T // 64], F32)
        d["sg"] = dscr("s%d_sg" % i, [4, 128, T], BF16)
        d["gt"] = dscr("s%d_gt" % i, [16, 128, T], BF16)
        d["of"] = dscr("s%d_of" % i, [2, 4, 128, T], BF16)
        d["oa"] = dscr("s%d_oa" % i, [4, 128, T], BF16)
        d["x2"] = dscr("s%d_x2" % i, [T, D], F32)
        d["h2"] = dscr("s%d_h2" % i, [8, 128, T], BF16)
        sc.append(d)

    ps = B.ps
    RP = ["ps%d" % i for i in range(8)]

    def phaseA():
        B.reset()
        wA = B.alloc([128, 8, INW], BF16)
        xin = [B.alloc([128, 2, D], F32) for _ in range(2)]
        hT = [B.alloc([128, 8, 512], BF16) for _ in range(2)]
        xn = [B.alloc([128, D], BF16) for _ in range(2)]
        junk = B.alloc([128, D], BF16)
        cs = [B.alloc([128, 2, 512], F32) for _ in range(2)]
        sq = [B.alloc([128, 512], F32) for _ in range(4)]
        NR = 3
        ffb = [B.alloc([128, 512], F32) for _ in range(NR)]
        lb_ = [B.alloc([128, 512], F32) for _ in range(NR)]
        gb_ = [B.alloc([128, 512], F32) for _ in range(NR)]
        epb = [B.alloc([128, 512], F32) for _ in range(NR)]
        gsm = [B.alloc([128, 8], F32) for _ in range(NR)]
        kgb = [[B.alloc([128, 512], BF16) for _ in range(4)] for _ in range(2)]
        NST = 6
        stg = [B.alloc([128, 512], BF16) for _ in range(NST)]
        qb = [B.alloc([128, 512], BF16) for _ in range(2)]
        t1b = [B.alloc([128, 512], F32) for _ in range(3)]
        t2b = [B.alloc([128, 512], F32) for _ in range(2)]
        ident = B.alloc([128, 128], BF16)
        rt = B.alloc([128, 128], BF16)
        reset = B.alloc([128, 2, 512], F32)
        g1 = B.alloc([128, 8], F32)
        lgt = B.alloc([128, 2, 2, 4], F32)
        lbv = B.alloc([128, 8], F32)
        oml = B.alloc([128, 8], F32)
        ss = [B.alloc([128, 4], F32) for _ in range(2)]
        rstd = [B.alloc([128, 4], F32) for _ in range(2)]
        dbuf = [B.alloc([128, 2, 4, 8], F32) for _ in range(2)]
        print("phaseA sbuf bytes", B.off)

        B.dma(ident, c_ident, "cst", W=["ident"])
        B.dma(rt, c_rt, "cst", W=["rt"])
        B.dma(reset, c_reset, "cst", W=["reset"])
        B.dma(g1, norm1.rearrange("(c p) -> p c", p=128), "cst", W=["g1"], allow_slow_non_contiguous=True)
        for dr in range(2):
            for l in range(2):
                B.dma(lgt[:, dr, l, :], lbl[dr, l, :].rearrange("(h p) -> p h", p=128), "cst", W=["lgt"],
                      allow_slow_non_contiguous=True)
        for dr in range(2):
            B.dve(_tt(lbv[:, dr * 4:(dr + 1) * 4], lgt[:, dr, 0, :], lgt[:, dr, 1, :], ALU.subtract),
                  R=["lgt"], W=["lbv%d" % dr])
        B.act(lbv, lbv, AF.Sigmoid, R=["lbv0", "lbv1"], W=["lbv"])
        B.dve(_ts(oml, lbv, -1.0, 1.0, ALU.mult, ALU.add), R=["lbv"], W=["oml"])

        wi = 0
        for kc in range(8):
            for q in range(3):
                sl = wi % 2
                st = xin[sl].rearrange("p a b -> p (a b)")
                B.dma(st, w_in[kc * 128:(kc + 1) * 128, q * 2048:(q + 1) * 2048], "xin%d" % sl,
                      W=["xin%d" % sl])
                B.dve(_ts(wA[:, kc, q * 2048:(q + 1) * 2048], st, g1[:, kc:kc + 1], None, ALU.mult),
                      R=["xin%d" % sl, "g1"], W=["wA"])
                wi += 1

        blocks = [(i, b) for i, T in enumerate(TS) for b in range(T // 512)]
        cnt = {"st": 0, "mb": 0, "r": 0, "kb": 0}

        def load_block(n):
            i, b = blocks[n]
            sl = n % 2
            B.dma(cs[sl][:, 0, :], c_cos[:, b * 512:(b + 1) * 512], "cs%d" % sl, W=["cs%d" % sl])
            B.dma(cs[sl][:, 1, :], c_sin[:, b * 512:(b + 1) * 512], "cs%d" % sl, W=["cs%d" % sl])

        def load_x(n, half):
            i, b = blocks[n]
            sl = half
            B.dma(xin[sl], xs[i][b * 512 + half * 256: b * 512 + (half + 1) * 256, :].rearrange(
                "(j p) d -> p j d", p=128), "xin%d" % sl, W=["xin%d" % sl])

        def next_stage():
            k = cnt["st"] % NST
            cnt["st"] += 1
            return k

        def mainbank():
            k = 2 + cnt["mb"] % 3
            cnt["mb"] += 1
            return k

        def norm_stats(n):
            sl = n % 2
            for half in range(2):
                xv = xin[half]
                for jj in range(2):
                    j = half * 2 + jj
                    B.act(junk, xv[:, jj, :], AF.Square, R=["xin%d" % half], W=["junk", "ss%d_%d" % (sl, j)],
                          scale=1.0 / 32.0, accum_out=ss[sl][:, j:j + 1])
            B.act(rstd[sl], ss[sl], AF.Ln, R=["ss%d_%d" % (sl, j) for j in range(4)], W=["rstdl%d" % sl],
                  bias=NORM_EPS, scale=1.0)
            B.act(rstd[sl], rstd[sl], AF.Exp, R=["rstdl%d" % sl], W=["rstd%d" % sl], scale=-0.5)

        def norm_tile(n, j, defer):
            sl = n % 2
            half, jj = j // 2, j % 2
            xs_ = j % 2
            B.dve(_ts(xn[xs_], xin[half][:, jj, :], rstd[sl][:, j:j + 1], None, ALU.mult),
                  R=["xin%d" % half, "rstd%d" % sl], W=["xn%d" % xs_])

            def trans():
                tb = j % 2
                pv = B.psb(tb)
                for kc in range(8):
                    B.tr(pv[:, kc * 128:(kc + 1) * 128], xn[xs_][:, kc * 128:(kc + 1) * 128], ident,
                         R=["xn%d" % xs_, "ident"], W=[RP[tb]])

                def ev():
                    B.act(hT[sl][:, :, j * 128:(j + 1) * 128], pv.rearrange("p (a b) -> p a b", b=128), AF.Copy,
                          R=[], W=[RP[tb], "hT%d" % sl])
                if defer is None:
                    ev()
                else:
                    defer(2, ev)
            if defer is None:
                trans()
            else:
                defer(2, trans)
            if jj == 1 and n + 1 < len(blocks):
                load_x(n + 1, half)

        def chunk_mm(sl, fc, bank):
            for kc in range(8):
                B.mm(ps[bank], wA[:, kc, fc * 128:(fc + 1) * 128], hT[sl][:, kc, :],
                     start=(kc == 0), stop=(kc == 7), R=["wA", "hT%d" % sl], W=[RP[bank]])

        def tok_mm(sl, j, col0, bank):
            for kc in range(8):
                B.mm(ps[bank], hT[sl][:, kc, j * 128:(j + 1) * 128], wA[:, kc, col0:col0 + 512],
                     start=(kc == 0), stop=(kc == 7), R=["wA", "hT%d" % sl], W=[RP[bank]])

        def compute_block(n):
            i, b = blocks[n]
            sl = n % 2
            d = sc[i]
            tsl = slice(b * 512, (b + 1) * 512)
            deferred = []
            forcing = [False]

            def defer(delay, fn):
                if forcing[0]:
                    fn()
                else:
                    deferred.append([delay, fn])

            def flush(force=False):
                if force:
                    forcing[0] = True
                k = 0
                while k < len(deferred):
                    deferred[k][0] -= 1
                    if deferred[k][0] <= 0 or force:
                        deferred.pop(k)[1]()
                    else:
                        k += 1
                forcing[0] = False

            def t_qhg(h):
                bank = mainbank()
                chunk_mm(sl, 12 + h, bank)
                flush()
                r = cnt["r"] % NR
                cnt["r"] += 1
                B.act(lb_[r], ps[bank], AF.Sigmoid, R=[RP[bank]], W=["l%d" % r])

                def s1():
                    B.dve(_stt(sq[h], ps[bank], -HG_SCALE, lb_[r], ALU.mult, ALU.mult), R=["l%d" % r],
                          W=[RP[bank], "sq%d" % h])
                defer(1, s1)

            def t_f(dr, h):
                bank = mainbank()
                chunk_mm(sl, 16 + dr * 4 + h, bank)
                flush()
                r = cnt["r"] % NR
                cnt["r"] += 1
                col = dr * 4 + h
                B.act(ffb[r], ps[bank], AF.Sigmoid, R=[], W=[RP[bank], "ff%d" % r])

                def s1():
                    B.dve(_ts(ffb[r], ffb[r], oml[:, col:col + 1], lbv[:, col:col + 1], ALU.mult, ALU.add),
                          R=["oml", "lbv", "ff%d" % r], W=["ff%d" % r])

                def s2():
                    if dr == 0:
                        B.dve(lambda e: e.tensor_tensor_scan(out=epb[r], data0=reset[:, 0, :], data1=ffb[r],
                                                             initial=0.0, op0=ALU.max, op1=ALU.mult),
                              R=["ff%d" % r, "reset"], W=["ep%d" % r])
                    else:
                        B.dve(lambda e: e.tensor_tensor_scan(out=epb[r][:, ::-1], data0=reset[:, 1, ::-1],
                                                             data1=ffb[r][:, ::-1], initial=0.0,
                                                             op0=ALU.max, op1=ALU.mult),
                              R=["ff%d" % r, "reset"], W=["ep%d" % r])

                def s3():
                    B.dve(lambda e: e.reciprocal(out=gb_[r], in_=epb[r]), R=["ep%d" % r], W=["g%d" % r])
                    if dr == 0:
                        B.pool(_cp(dbuf[sl][:, dr, h, :], epb[r][:, 63::64]), R=["ep%d" % r], W=["dbuf%d" % sl])
                    else:
                        B.pool(_cp(dbuf[sl][:, dr, h, :], epb[r][:, 0::64]), R=["ep%d" % r], W=["dbuf%d" % sl])
                    k = next_stage()
                    B.pool(_tt(stg[k], sq[h], epb[r], ALU.mult), R=["sq%d" % h, "ep%d" % r], W=["stg%d" % k])
                    B.dma(d["qg"][dr, h, :, tsl], stg[k], "stg%d" % k, R=["stg%d" % k], W=[])

                def s4():
                    B.pool(_ts(ffb[r], ffb[r], 1.0, -1.0, ALU.mult, ALU.add), R=["ff%d" % r], W=["ff%d" % r])

                def s5():
                    B.pool(_tt(kgb[dr][h], ffb[r], gb_[r], ALU.mult), R=["ff%d" % r, "g%d" % r],
                           W=["kgb%d%d" % (dr, h)])
                    B.dma(d["kg"][dr, h, :, tsl], kgb[dr][h], "kgb%d%d" % (dr, h), R=["kgb%d%d" % (dr, h)], W=[])
                    if h == 3 and dr == 1:
                        for d2 in range(2):
                            B.dma(d["dd"][d2, :, :, b * 8:(b + 1) * 8], dbuf[sl][:, d2, :, :], "dbuf%d" % sl,
                                  R=["dbuf%d" % sl], W=[], allow_slow_non_contiguous=True)

                for k_, f_ in enumerate((s1, s2, s3, s4, s5)):
                    defer(k_ + 1, f_)
                if h == 3:
                    for j in range(4):
                        def ktrans(dr=dr, j=j):
                            pv = B.psb(7)
                            for hh in range(4):
                                B.tr(pv[:, hh * 128:(hh + 1) * 128], kgb[dr][hh][:, j * 128:(j + 1) * 128], ident,
                                     R=["kgb%d%d" % (dr, hh), "ident"], W=[RP[7]])

                            def kev():
                                k2 = next_stage()
                                B.dve(_cp(stg[k2], pv[:, 0:512]), R=[], W=[RP[7], "stg%d" % k2])
                                B.dma(d["kt"][dr, b * 512 + j * 128: b * 512 + (j + 1) * 128, :], stg[k2],
                                      "stg%d" % k2, R=["stg%d" % k2], W=[])
                            defer(2, kev)
                        defer(7 + 2 * j, ktrans)

            def t_qk(fc):
                bank = mainbank()
                chunk_mm(sl, fc, bank)
                flush()
                r = fc % 2
                r3 = fc % 3
                rb = 5 + fc % 2
                B.act(qb[r], ps[bank], AF.Copy, R=[], W=[RP[bank], "qb%d" % r])

                def s1():
                    B.mm(ps[rb], rt, qb[r], R=["rt", "qb%d" % r], W=[RP[rb]])
                    B.pool(_tt(t1b[r3], qb[r], cs[sl][:, 0, :], ALU.mult), R=["qb%d" % r, "cs%d" % sl],
                           W=["t1%d" % r3])

                def s2():
                    B.dve(_tt(t2b[r], ps[rb], cs[sl][:, 1, :], ALU.mult), R=["cs%d" % sl], W=[RP[rb], "t2%d" % r])

                def s3():
                    k = next_stage()
                    B.pool(_tt(stg[k], t1b[r3], t2b[r], ALU.add), R=["t1%d" % r3, "t2%d" % r], W=["stg%d" % k])
                    dst = d["qT"] if fc < 4 else d["kT"]
                    B.dma(dst[fc % 4, :, tsl], stg[k], "stg%d" % k, R=["stg%d" % k], W=[])
                for k_, f_ in enumerate((s1, s2, s3)):
                    defer(k_ + 1, f_)

            def t_v(j):
                bank = mainbank()
                tok_mm(sl, j, 1024, bank)
                flush()
                k = next_stage()
                B.act(stg[k], ps[bank], AF.Copy, R=[], W=[RP[bank], "stg%d" % k])
                B.dma(d["vA"][:, :, b * 4 + j, :].rearrange("h p v -> p h v"),
                      stg[k].rearrange("p (h v) -> p h v", v=128), "stg%d" % k, R=["stg%d" % k], W=[])

            def t_i(j):
                bank = mainbank()
                tok_mm(sl, j, 3072, bank)
                flush()
                k = next_stage()
                B.act(stg[k], ps[bank], AF.Copy, R=[], W=[RP[bank], "stg%d" % k])
                B.dma(d["vg"][b * 512 + j * 128: b * 512 + (j + 1) * 128, :], stg[k], "stg%d" % k,
                      R=["stg%d" % k], W=[])

            def t_ghg(h):
                bank = mainbank()
                chunk_mm(sl, 28 + h, bank)
                flush()
                r = cnt["r"] % NR
                cnt["r"] += 1
                B.act(lb_[r], ps[bank], AF.Sigmoid, R=[RP[bank]], W=["l%d" % r])

                def s1():
                    k = next_stage()
                    B.dve(_tt(stg[k], ps[bank], lb_[r], ALU.mult), R=["l%d" % r], W=[RP[bank], "stg%d" % k])
                    B.dma(d["sg"][h, :, tsl], stg[k], "stg%d" % k, R=["stg%d" % k], W=[])
                defer(1, s1)

            def t_gate(c):
                bank = mainbank()
                chunk_mm(sl, 32 + c, bank)
                flush()
                k = next_stage()
                B.act(stg[k], ps[bank], AF.Sigmoid, R=[], W=[RP[bank], "stg%d" % k])
                B.dma(d["gt"][c, :, tsl], stg[k], "stg%d" % k, R=["stg%d" % k], W=[])

            others = ([(t_ghg, (h,)) for h in range(4)] + [(t_gate, (c,)) for c in range(16)]
                      + [(t_qk, (fc,)) for fc in range(8)] + [(t_v, (j,)) for j in range(4)]
                      + [(t_i, (j,)) for j in range(4)])
            order = [(t_qhg, (h,)) for h in range(4)]
            oi = 0
            for dr in range(2):
                for h in range(4):
                    order.append((t_f, (dr, h)))
                    order += others[oi:oi + 4]
                    oi += 4
            order += others[oi:]
            have_next = n + 1 < len(blocks)
            if have_next:
                load_block(n + 1)
                norm_stats(n + 1)
            for pos, (fn, args) in enumerate(order):
                fn(*args)
                if have_next and pos in (20, 24, 28, 32):
                    norm_tile(n + 1, (pos - 20) // 4, defer)
            flush(force=True)

        load_block(0)
        load_x(0, 0)
        load_x(0, 1)
        norm_stats(0)
        for j in range(4):
            norm_tile(0, j, None)
        for n in range(len(blocks)):
            compute_block(n)

    def phaseB1(i):
        B.reset()
        T = TS[i]
        d = sc[i]
        nblk = T // 512
        nch = T // 64
        Sst = [B.alloc([128, 512], F32) for _ in range(2)]
        Sb = [B.alloc([128, 512], BF16) for _ in range(2)]
        T1 = [B.alloc([128, 512], F32) for _ in range(2)]
        Ab = [B.alloc([64, 256], BF16) for _ in range(2)]
        gmask = B.alloc([64, 2, 256], F32)
        qg = [[B.alloc([128, 4, 512], BF16) for _ in range(2)] for _ in range(2)]
        kg = [[B.alloc([128, 4, 512], BF16) for _ in range(2)] for _ in range(2)]
        kt = [[B.alloc([64, 8, 512], BF16) for _ in range(2)] for _ in range(2)]
        vg = [[B.alloc([64, 8, 512], BF16) for _ in range(2)] for _ in range(2)]
        dd = [[B.alloc([128, 4, 8], F32) for _ in range(2)] for _ in range(2)]
        ob = [[B.alloc([128, 4, 512], BF16) for _ in range(2)] for _ in range(2)]
        for dr in range(2):
            B.dma(gmask[:, dr, :], c_gmask[dr], "cst", W=["gmask"])

        def load(dr, bi):
            blk = bi if dr == 0 else nblk - 1 - bi
            sl = bi % 2
            tsl = slice(blk * 512, (blk + 1) * 512)
            nm = "b1in%d%d" % (dr, sl)
            B.dma(qg[dr][sl], d["qg"][dr, :, :, tsl].rearrange("h p t -> p h t"), nm, W=[nm])
            B.dma(kg[dr][sl], d["kg"][dr, :, :, tsl].rearrange("h p t -> p h t"), nm, W=[nm])
            B.dma(kt[dr][sl], d["kt"][dr, tsl, :].rearrange("(c p) f -> p c f", p=64), nm, W=[nm])
            B.dma(vg[dr][sl], d["vg"][tsl, :].rearrange("(c p) f -> p c f", p=64), nm, W=[nm])
            B.dma(dd[dr][sl], d["dd"][dr, :, :, blk * 8:(blk + 1) * 8], nm, W=[nm], allow_slow_non_contiguous=True)

        def idx(dr, st):
            bi = st // 8
            ci = st % 8
            blk = bi if dr == 0 else nblk - 1 - bi
            cc = ci if dr == 0 else 7 - ci
            return bi, ci, blk, cc, bi % 2

        def emit_A(dr, st):
            bi, ci, blk, cc, sl = idx(dr, st)
            nm = "b1in%d%d" % (dr, sl)
            pA, psA = RP[dr * 3], ps[dr * 3]
            csl = slice(cc * 64, (cc + 1) * 64)
            for h in range(4):
                B.mm(psA[0:64, h * 64:(h + 1) * 64], kg[dr][sl][:, h, csl], qg[dr][sl][:, h, csl],
                     R=[nm], W=[pA])
            B.dve(_tt(Ab[dr], psA[0:64, 0:256], gmask[:, dr, :], ALU.mult), R=["gmask"], W=[pA, "Ab%d" % dr])

        def emit_main(dr, st):
            bi, ci, blk, cc, sl = idx(dr, st)
            nm = "b1in%d%d" % (dr, sl)
            first = (st == 0)
            pO, pP = RP[dr * 3 + 1], RP[dr * 3 + 2]
            psO, psP = ps[dr * 3 + 1], ps[dr * 3 + 2]
            csl = slice(cc * 64, (cc + 1) * 64)
            for h in range(4):
                B.mm(psP[:, h * 128:(h + 1) * 128], kt[dr][sl][:, cc, h * 128:(h + 1) * 128],
                     vg[dr][sl][:, cc, h * 128:(h + 1) * 128], R=[nm], W=[pP])
            if first:
                B.dve(_cp(T1[dr], psP), R=[], W=[pP, "T1%d" % dr])
            else:
                B.dve(_tt(T1[dr], psP, Sst[dr], ALU.add), R=["S%d" % dr], W=[pP, "T1%d" % dr])
            for h in range(4):
                B.mm(psO[:, h * 64:(h + 1) * 64], vg[dr][sl][:, cc, h * 128:(h + 1) * 128],
                     Ab[dr][:, h * 64:(h + 1) * 64], start=True, stop=first, R=[nm, "Ab%d" % dr], W=[pO])
                if not first:
                    sbn = ("Sba%d" % dr) if h < 2 else ("Sbb%d_%d" % (dr, h))
                    B.mm(psO[:, h * 64:(h + 1) * 64], Sb[dr][:, h * 128:(h + 1) * 128], qg[dr][sl][:, h, csl],
                         start=False, stop=True, R=[nm, sbn], W=[pO])
            onm = "ob%d%d" % (dr, sl)
            dbc = dd[dr][sl][:, :, cc:cc + 1].broadcast_to([128, 4, 128])
            dbc2 = dd[dr][sl][:, 0:2, cc:cc + 1].broadcast_to([128, 2, 128])
            B.dve(_tt(Sb[dr][:, 0:256].rearrange("p (h v) -> p h v", v=128),
                      T1[dr][:, 0:256].rearrange("p (h v) -> p h v", v=128), dbc2, ALU.mult),
                  R=["T1%d" % dr, nm], W=["Sba%d" % dr])
            for h in (2, 3):
                B.act(Sb[dr][:, h * 128:(h + 1) * 128], T1[dr][:, h * 128:(h + 1) * 128], AF.Copy,
                      R=["T1%d" % dr, nm], W=["Sbb%d_%d" % (dr, h)], scale=dd[dr][sl][:, h, cc:cc + 1])
            B.act(ob[dr][sl][:, :, csl], psO[:, 0:256].rearrange("p (h t) -> p h t", t=64), AF.Copy,
                  R=[], W=[pO, onm])
            B.pool(_tt(Sst[dr].rearrange("p (h v) -> p h v", v=128), T1[dr].rearrange("p (h v) -> p h v", v=128),
                       dbc, ALU.mult), R=["T1%d" % dr, nm], W=["S%d" % dr])
            if ci == 7:
                tsl = slice(blk * 512, (blk + 1) * 512)
                B.dma(d["of"][dr, :, :, tsl].rearrange("h p t -> p h t"), ob[dr][sl], onm, R=[onm], W=[])

        for dr in range(2):
            load(dr, 0)
        for dr in range(2):
            emit_A(dr, 0)
        for st in range(nch):
            if st % 8 == 0 and st // 8 + 1 < nblk:
                for dr in range(2):
                    load(dr, st // 8 + 1)
            for dr in range(2):
                emit_main(dr, st)
            if st + 1 < nch:
                for dr in range(2):
                    emit_A(dr, st + 1)

    def phaseB2(i):
        B.reset()
        T = TS[i]
        d = sc[i]
        NKT = T // 128
        NQB = T // 512
        KT = [B.alloc([128, T], BF16) for _ in range(2)]
        QT = [B.alloc([128, T], BF16) for _ in range(2)]
        VV = [B.alloc([128, NKT, 128], BF16) for _ in range(2)]
        NP = 4
        pT2 = [B.alloc([128, 1024], BF16) for _ in range(NP)]
        pp2 = [B.alloc([128, 1024], BF16) for _ in range(2)]
        lrows = B.alloc([64, 512], F32)
        ones = B.alloc([128, 128], BF16)
        inv32 = B.alloc([64, 128], F32)
        osb = [B.alloc([128, 512], F32) for _ in range(2)]
        on = [B.alloc([128, 512], F32) for _ in range(2)]
        diff = B.alloc([128, 512], F32)
        sqd = B.alloc([128, 512], BF16)
        lnv = B.alloc([128, 512], F32)
        ost = [B.alloc([128, 512], BF16) for _ in range(2)]
        lv = B.alloc([128, 4, 64], F32)
        lt = B.alloc([128, 2, 64], F32)
        ls = B.alloc([128, 2], F32)
        neglam = B.alloc([128, 1], F32)
        gsub = B.alloc([128, 1], F32)
        B.dma(ones, c_ones, "cst", W=["ones"])
        B.dma(inv32, c_inv32, "cst", W=["inv32"])
        for k, v in enumerate([lq1, lk1, lq2, lk2]):
            B.dma(lv[:, k, :], v.partition_broadcast(128), "cst", W=["lv"])
        B.dma(gsub, subln.rearrange("(p o) -> p o", o=1), "cst", W=["gsub0"])
        B.dve(_ts(gsub, gsub, 1.0 - LAM_INIT, None, ALU.mult), R=["gsub0"], W=["gsub"])
        for k in range(2):
            B.dve(_tt(lt[:, k, :], lv[:, 2 * k, :], lv[:, 2 * k + 1, :], ALU.mult), R=["lv"], W=["lt%d" % k])
            B.dve(lambda e, k=k: e.reduce_sum(out=ls[:, k:k + 1], in_=lt[:, k, :], axis=mybir.AxisListType.X),
                  R=["lt%d" % k], W=["ls%d" % k])
        B.act(ls, ls, AF.Exp, R=["ls0", "ls1"], W=["lse"])
        B.dve(_tt(neglam, ls[:, 1:2], ls[:, 0:1], ALU.subtract), R=["lse"], W=["nl0"])
        B.dve(_ts(neglam, neglam, -LAM_INIT, None, ALU.add), R=["nl0"], W=["neglam"])

        def load_head(h):
            sl = h % 2
            nm = "hd%d" % sl
            B.dma(KT[sl], d["kT"][h], nm, W=[nm])
            B.dma(QT[sl], d["qT"][h], nm, W=[nm])
            for g in range(0, NKT, 16):
                ge = min(NKT, g + 16)
                B.dma(VV[sl][:, g:ge, :], d["vA"][h, :, g:ge, :], nm, W=[nm])

        cnt = {"p": 0, "s": 0, "o": 0, "q": 0}
        prev = [0]
        psall = B.psall

        def head(h):
            sl = h % 2
            nm = "hd%d" % sl
            fin = []
            for qbk in range(NQB):
                qsl = slice(qbk * 512, (qbk + 1) * 512)
                pend = []
                sumq = []

                def qk(kt_):
                    g = cnt["s"] % 2
                    cnt["s"] += 1
                    b0, b1 = 2 * g, 2 * g + 1
                    ksl = slice(kt_ * 128, (kt_ + 1) * 128)
                    B.mm(ps[b0], KT[sl][0:64, ksl], QT[sl][0:64, qsl], R=[nm], W=[RP[b0]])
                    B.mm(ps[b1], KT[sl][64:128, ksl], QT[sl][64:128, qsl], R=[nm], W=[RP[b1]])
                    pi = cnt["p"] % NP
                    cnt["p"] += 1
                    B.act(pT2[pi], psall[:, b0 * 512:(b0 + 2) * 512], AF.Exp, R=[], W=[RP[b0], RP[b1], "pT%d" % pi],
                          scale=DA_SCALE)
                    pend.append((kt_, pi))

                def sums():
                    q2, kt_ = sumq.pop(0)
                    for c in range(2):
                        B.mm(ps[6][32 * c:32 * c + 32, :], ones[:, 0:32], pp2[q2][:, c * 512:(c + 1) * 512],
                             start=(kt_ == 1), stop=(kt_ == NKT - 1), R=["ones", "pp%d" % q2], W=[RP[6]])

                def pv():
                    kt_, pi = pend.pop(0)
                    for c in range(2):
                        B.mm(ps[4 + c], VV[sl][:, kt_, :], pT2[pi][:, c * 512:(c + 1) * 512], start=(kt_ == 0),
                             stop=(kt_ == NKT - 1), R=[nm, "pT%d" % pi], W=[RP[4 + c]])
                    if sumq:
                        sums()
                    if kt_ % 2 == 0:
                        prev[0] = pi
                    else:
                        q2 = cnt["q"] % 2
                        cnt["q"] += 1
                        B.dve(_tt(pp2[q2], pT2[prev[0]], pT2[pi], ALU.add), R=["pT%d" % prev[0], "pT%d" % pi],
                              W=["pp%d" % q2])
                        sumq.append((q2, kt_))
                    if fin:
                        fin.pop(0)()

                for kt_ in range(NKT):
                    qk(kt_)
                    if len(pend) > 2:
                        pv()
                while pend:
                    pv()
                while sumq:
                    sums()
                while fin:
                    fin.pop(0)()
                for c in range(2):
                    B.dve(_cp(osb[c], ps[4 + c]), R=[], W=[RP[4 + c], "osb%d" % c])
                B.act(lrows, ps[6][0:64, :], AF.Copy, R=[], W=[RP[6], "lrows"])

                def f1():
                    B.dve(lambda e: e.reciprocal(out=lrows, in_=lrows), R=["lrows"], W=["lrows"])

                def f2(c):
                    def f():
                        B.mm(ps[7], inv32[32 * c:32 * c + 32, :], lrows[32 * c:32 * c + 32, :], R=["inv32", "lrows"],
                             W=[RP[7]])
                        B.dve(_tt(on[c], osb[c], ps[7], ALU.mult), R=["osb%d" % c], W=[RP[7], "on%d" % c])
                    return f

                def f4():
                    B.dve(_stt(diff, on[1], neglam[:, 0:1], on[0], ALU.mult, ALU.add), R=["on0", "on1", "neglam"],
                          W=["diff"])
                    B.act(sqd, diff, AF.Square, R=["diff"], W=["sqd"])

                def f5():
                    B.mm(ps[7], ones, sqd, R=["ones", "sqd"], W=[RP[7]])

                def f6():
                    B.act(lnv, ps[7], AF.Ln, R=[], W=[RP[7], "lnv"], scale=1.0 / 128.0, bias=SUBLN_EPS)
                    B.act(lnv, lnv, AF.Exp, R=["lnv"], W=["rsd"], scale=-0.5)

                def f7(qsl=qsl):
                    k = cnt["o"] % 2
                    cnt["o"] += 1
                    B.dve(_stt(ost[k], diff, gsub[:, 0:1], lnv, ALU.mult, ALU.mult), R=["diff", "gsub", "rsd"],
                          W=["ost%d" % k])
                    B.dma(d["oa"][h, :, qsl], ost[k], "ost%d" % k, R=["ost%d" % k], W=[])
                fin.extend([f1, f2(0), f2(1), f4, f5, f6, f7])
            while fin:
                fin.pop(0)()

        load_head(0)
        for h in range(4):
            if h + 1 < 4:
                load_head(h + 1)
            head(h)

    def phaseC1(i, first):
        T = TS[i]
        d = sc[i]
        st = C1
        if first:
            B.reset()
            st["whg"] = B.alloc([128, 4, D], BF16)
            st["wda"] = B.alloc([128, 4, D], BF16)
            st["wo"] = B.alloc([128, 8, D], BF16)
            st["of"] = [B.alloc([128, 2, 4, 512], BF16) for _ in range(2)]
            st["sg"] = [B.alloc([128, 4, 512], BF16) for _ in range(2)]
            st["oa"] = [B.alloc([128, 4, 512], BF16) for _ in range(2)]
            st["gt"] = [B.alloc([128, 16, 512], BF16) for _ in range(2)]
            st["x"] = [B.alloc([128, 4, D], F32) for _ in range(2)]
            st["osum"] = [B.alloc([128, 512], F32) for _ in range(2)]
            st["sqo"] = [B.alloc([128, 512], BF16) for _ in range(2)]
            st["rs"] = [B.alloc([128, 512], F32) for _ in range(2)]
            st["ohg"] = [B.alloc([128, 4, 512], BF16) for _ in range(2)]
            st["mT"] = B.alloc([128, 8, 512], BF16)
            st["t1"] = [B.alloc([128, 512], F32) for _ in range(2)]
            st["t2"] = [B.alloc([128, 512], F32) for _ in range(2)]
            st["x2s"] = [B.alloc([128, D], F32) for _ in range(3)]
            st["xn2"] = [B.alloc([128, D], BF16) for _ in range(2)]
            st["h2s"] = [B.alloc([128, 8, 128], BF16) for _ in range(2)]
            st["junk"] = B.alloc([128, D], BF16)
            st["ones"] = B.alloc([128, 128], BF16)
            st["ident"] = B.alloc([128, 128], BF16)
            st["ghg"] = B.alloc([128, 4], F32)
            st["ss"] = B.alloc([128, 4], F32)
            st["rstd"] = B.alloc([128, 4], F32)
            st["cnt"] = {"t": 0, "x": 0, "o": 0}
            st["deferred"] = []
            print("phaseC1 sbuf bytes", B.off)
            B.dma(st["ones"], c_ones, "cst", W=["ones"])
            B.dma(st["ident"], c_ident, "cst", W=["ident"])
            B.dma(st["ghg"], hg_norm.rearrange("(h p) -> p h", p=128), "cst", W=["ghg"], allow_slow_non_contiguous=True)
            wi = 0
            for (wsrc, wdst, nk) in ((w_hg, st["whg"], 4), (w_da, st["wda"], 4), (w_out, st["wo"], 8)):
                for kc in range(0, nk, 4):
                    sl = wi % 2
                    B.dma(st["x"][sl], wsrc[kc * 128:(kc + 4) * 128, :].rearrange("(c p) d -> p c d", p=128),
                          "c1in%d" % sl, W=["c1in%d" % sl])
                    B.dve(_cp(wdst[:, kc:kc + 4, :], st["x"][sl]), R=["c1in%d" % sl], W=["wC1"])
                    wi += 1
        whg, wda, wo = st["whg"], st["wda"], st["wo"]
        ones, ident, ghg = st["ones"], st["ident"], st["ghg"]
        nblk = T // 512
        cnt = st["cnt"]
        deferred = st["deferred"]

        def defer(delay, fn):
            deferred.append([delay, fn])

        def flush(force=False):
            k = 0
            while k < len(deferred):
                deferred[k][0] -= 1
                if deferred[k][0] <= 0 or force:
                    deferred.pop(k)[1]()
                else:
                    k += 1

        def load(b):
            sl = b % 2
            nm = "c1in%d" % sl
            tsl = slice(b * 512, (b + 1) * 512)
            for dr in range(2):
                B.dma(st["of"][sl][:, dr, :, :], d["of"][dr, :, :, tsl].rearrange("h p t -> p h t"), nm, W=[nm])
            B.dma(st["sg"][sl], d["sg"][:, :, tsl].rearrange("h p t -> p h t"), nm, W=[nm])
            B.dma(st["oa"][sl], d["oa"][:, :, tsl].rearrange("h p t -> p h t"), nm, W=[nm])
            B.dma(st["gt"][sl], d["gt"][:, :, tsl].rearrange("h p t -> p h t"), nm, W=[nm])
            B.dma(st["x"][sl], xs[i][tsl, :].rearrange("(j p) d -> p j d", p=128), nm, W=[nm])

        def prologue_head(b, h):
            sl = b % 2
            nm = "c1in%d" % sl
            r = cnt["o"] % 2
            cnt["o"] += 1
            osum, sqo, rs = st["osum"][r], st["sqo"][r], st["rs"][r]
            ohg = st["ohg"][sl]
            B.pool(_tt(osum, st["of"][sl][:, 0, h, :], st["of"][sl][:, 1, h, :], ALU.add), R=[nm], W=["osum%d" % r])
            B.act(sqo, osum, AF.Square, R=["osum%d" % r], W=["sqo%d" % r])

            def rest():
                B.mm(ps[7], ones, sqo, R=["ones", "sqo%d" % r], W=[RP[7]])
                B.act(rs, ps[7], AF.Ln, R=[], W=[RP[7], "rsl%d" % r], scale=1.0 / 128.0, bias=NORM_EPS)
                B.act(rs, rs, AF.Exp, R=["rsl%d" % r], W=["rs%d" % r], scale=-0.5)
                B.dve(_tt(osum, osum, rs, ALU.mult), R=["rs%d" % r, "osum%d" % r], W=["osum%d" % r])
                B.dve(_stt(ohg[:, h, :], osum, ghg[:, h:h + 1], st["sg"][sl][:, h, :], ALU.mult, ALU.mult),
                      R=["osum%d" % r, "ghg", nm], W=["ohg%d" % sl])
            return rest

        def proj_chunk(b, dc):
            sl = b % 2
            nm = "c1in%d" % sl
            mT = st["mT"]
            r = cnt["t"] % 2
            cnt["t"] += 1
            ba, bb = 0 + 2 * r, 1 + 2 * r
            for kc in range(4):
                B.mm(ps[ba], whg[:, kc, dc * 128:(dc + 1) * 128], st["ohg"][sl][:, kc, :], start=(kc == 0),
                     stop=(kc == 3), R=["wC1", "ohg%d" % sl], W=[RP[ba]])
            for kc in range(4):
                B.mm(ps[bb], wda[:, kc, dc * 128:(dc + 1) * 128], st["oa"][sl][:, kc, :], start=(kc == 0),
                     stop=(kc == 3), R=["wC1", nm], W=[RP[bb]])
            B.dve(_tt(st["t1"][r], ps[ba], st["gt"][sl][:, dc, :], ALU.mult), R=[nm], W=[RP[ba], "t1%d" % r])
            B.dve(_tt(st["t2"][r], ps[bb], st["gt"][sl][:, 8 + dc, :], ALU.mult), R=[nm], W=[RP[bb], "t2%d" % r])
            B.pool(_tt(mT[:, dc, :], st["t1"][r], st["t2"][r], ALU.add), R=["t1%d" % r, "t2%d" % r], W=["mT"])

        def out_tile(b, j):
            sl = b % 2
            nm = "c1in%d" % sl
            mT = st["mT"]
            xr = cnt["x"] % 3
            hr = cnt["x"] % 2
            sr = cnt["x"] % 4
            cnt["x"] += 1
            x2s = st["x2s"][xr]
            for half in range(2):
                bank = 4 + half
                for kc in range(8):
                    B.mm(ps[bank], mT[:, kc, j * 128:(j + 1) * 128], wo[:, kc, half * 512:(half + 1) * 512],
                         start=(kc == 0), stop=(kc == 7), R=["mT", "wC1"], W=[RP[bank]])
                B.dve(_tt(x2s[:, half * 512:(half + 1) * 512], ps[bank],
                          st["x"][sl][:, j, half * 512:(half + 1) * 512], ALU.add), R=[nm],
                      W=[RP[bank], "x2s%d" % xr])
            B.dma(d["x2"][b * 512 + j * 128: b * 512 + (j + 1) * 128, :], x2s, "x2s%d" % xr, R=["x2s%d" % xr], W=[])
            B.act(st["junk"], x2s, AF.Square, R=["x2s%d" % xr], W=["junk", "ssc%d" % sr], scale=1.0 / 32.0,
                  accum_out=st["ss"][:, sr:sr + 1])
            B.act(st["rstd"][:, sr:sr + 1], st["ss"][:, sr:sr + 1], AF.Ln, R=["ssc%d" % sr], W=["rsc%d" % sr],
                  bias=NORM_EPS, scale=1.0)
            B.act(st["rstd"][:, sr:sr + 1], st["rstd"][:, sr:sr + 1], AF.Exp, R=["rsc%d" % sr], W=["rstdc%d" % sr],
                  scale=-0.5)
            B.dve(_ts(st["xn2"][hr], x2s, st["rstd"][:, sr:sr + 1], None, ALU.mult),
                  R=["x2s%d" % xr, "rstdc%d" % sr], W=["xn2%d" % hr])

            def trans():
                pv = B.psb(6)
                for kc in range(8):
                    B.tr(pv[:, kc * 128:(kc + 1) * 128], st["xn2"][hr][:, kc * 128:(kc + 1) * 128], ident,
                         R=["xn2%d" % hr, "ident"], W=[RP[6]])
                B.act(st["h2s"][hr], pv.rearrange("p (a b) -> p a b", b=128), AF.Copy, R=[], W=[RP[6], "h2s%d" % hr])
                B.dma(d["h2"][:, :, b * 512 + j * 128: b * 512 + (j + 1) * 128].rearrange("c p t -> p c t"),
                      st["h2s"][hr], "h2s%d" % hr, R=["h2s%d" % hr], W=[])
            defer(2, trans)

        load(0)
        for h in range(4):
            prologue_head(0, h)()
        for b in range(nblk):
            nxt = b + 1 < nblk
            if nxt:
                load(b + 1)
            for dc in range(8):
                proj_chunk(b, dc)
                flush()
            rests = {}
            out_tile(b, 0)
            flush()
            if nxt:
                rests[0] = prologue_head(b + 1, 0)
                rests[1] = prologue_head(b + 1, 1)
            out_tile(b, 1)
            flush()
            if nxt:
                rests[0]()
                rests[1]()
                rests[2] = prologue_head(b + 1, 2)
                rests[3] = prologue_head(b + 1, 3)
            out_tile(b, 2)
            flush()
            if nxt:
                rests[2]()
                rests[3]()
            out_tile(b, 3)
            flush()
        flush(force=True)

    def phaseC2(i, first):
        T = TS[i]
        d = sc[i]
        st = C2
        if first:
            B.reset()
            st["w1"] = B.alloc([128, 8, DFF], BF16)
            st["w2"] = B.alloc([128, 32, D], BF16)
            st["h2"] = [B.alloc([128, 8, 256], BF16) for _ in range(2)]
            st["x2"] = [B.alloc([128, 2, D], F32) for _ in range(2)]
            st["rl"] = [B.alloc([128, 256], F32) for _ in range(3)]
            st["hid"] = [B.alloc([128, 256], BF16) for _ in range(4)]
            st["x3"] = [B.alloc([128, D], F32) for _ in range(2)]
            st["yo"] = [B.alloc([128, D], F32) for _ in range(2)]
            st["fng"] = B.alloc([128, D], F32)
            st["junk"] = B.alloc([128, D], BF16)
            st["g2"] = B.alloc([128, 8], F32)
            st["ss"] = B.alloc([128, 2], F32)
            st["rstd"] = B.alloc([128, 2], F32)
            B.dma(st["fng"], fnorm.partition_broadcast(128), "cst", W=["fng"])
            B.dma(st["g2"], norm2.rearrange("(c p) -> p c", p=128), "cst", W=["g2"], allow_slow_non_contiguous=True)
            wi = 0
            for kc in range(8):
                for q in range(2):
                    sl = wi % 2
                    stv = st["x2"][sl].rearrange("p a b -> p (a b)")
                    B.dma(stv, w1[kc * 128:(kc + 1) * 128, q * 2048:(q + 1) * 2048], "c2in%d" % sl, W=["c2in%d" % sl])
                    B.dve(_ts(st["w1"][:, kc, q * 2048:(q + 1) * 2048], stv, st["g2"][:, kc:kc + 1], None, ALU.mult),
                          R=["c2in%d" % sl, "g2"], W=["w1C2"])
                    wi += 1
            for fc in range(0, 32, 2):
                sl = wi % 2
                B.dma(st["x2"][sl], w2[fc * 128:(fc + 2) * 128, :].rearrange("(c p) d -> p c d", p=128),
                      "c2in%d" % sl, W=["c2in%d" % sl])
                B.act(st["w2"][:, fc:fc + 2, :], st["x2"][sl], AF.Copy, R=["c2in%d" % sl], W=["w2C2"])
                wi += 1
        W1, W2 = st["w1"], st["w2"]
        nblk = T // 256

        def load(b):
            sl = b % 2
            nm = "c2in%d" % sl
            tsl = slice(b * 256, (b + 1) * 256)
            B.dma(st["h2"][sl], d["h2"][:, :, tsl].rearrange("c p t -> p c t"), "c2h%d" % sl, W=["c2h%d" % sl])
            B.dma(st["x2"][sl], d["x2"][tsl, :].rearrange("(j p) d -> p j d", p=128), nm, W=[nm])

        cnt = {"h": 0, "r": 0, "y": 0}

        def block(b):
            sl = b % 2
            nm = "c2in%d" % sl
            hn = "c2h%d" % sl
            pend = []

            def l2():
                fc, hi = pend.pop(0)
                for j in range(2):
                    for half in range(2):
                        bank = j * 2 + half
                        B.mm(ps[bank][:, :], st["hid"][hi][:, j * 128:(j + 1) * 128],
                             W2[:, fc, half * 512:(half + 1) * 512], start=(fc == 0), stop=(fc == 31),
                             R=["hid%d" % hi, "w2C2"], W=[RP[bank]])

            for fc in range(32):
                bank = 4 + cnt["h"] % 3
                ri = cnt["h"] % 3
                hi = cnt["h"] % 4
                cnt["h"] += 1
                for kc in range(8):
                    B.mm(ps[bank][:, 0:256], W1[:, kc, fc * 128:(fc + 1) * 128], st["h2"][sl][:, kc, :],
                         start=(kc == 0), stop=(kc == 7), R=["w1C2", hn], W=[RP[bank]])
                B.act(st["rl"][ri], ps[bank][:, 0:256], AF.Relu, R=[], W=[RP[bank], "rl%d" % ri])
                B.pool(_tt(st["hid"][hi], st["rl"][ri], st["rl"][ri], ALU.mult), R=["rl%d" % ri], W=["hid%d" % hi])
                pend.append((fc, hi))
                if len(pend) > 2:
                    l2()
            while pend:
                l2()
            xrs = []
            for j in range(2):
                xr = cnt["y"] % 2
                cnt["y"] += 1
                xrs.append(xr)
                x3 = st["x3"][xr]
                for half in range(2):
                    bank = j * 2 + half
                    B.dve(_tt(x3[:, half * 512:(half + 1) * 512], ps[bank],
                              st["x2"][sl][:, j, half * 512:(half + 1) * 512], ALU.add), R=[nm],
                          W=[RP[bank], "x3%d" % xr])
            for j in range(2):
                xr = xrs[j]
                x3 = st["x3"][xr]
                B.act(st["junk"], x3, AF.Square, R=["x3%d" % xr], W=["junk", "ssd%d" % xr], scale=1.0 / 32.0,
                      accum_out=st["ss"][:, xr:xr + 1])
                B.act(st["rstd"][:, xr:xr + 1], st["ss"][:, xr:xr + 1], AF.Ln, R=["ssd%d" % xr], W=["rsd%d" % xr],
                      bias=NORM_EPS, scale=1.0)
                B.act(st["rstd"][:, xr:xr + 1], st["rstd"][:, xr:xr + 1], AF.Exp, R=["rsd%d" % xr], W=["rstdd%d" % xr],
                      scale=-0.5)
                B.dve(_stt(st["yo"][xr], x3, st["rstd"][:, xr:xr + 1], st["fng"], ALU.mult, ALU.mult),
                      R=["x3%d" % xr, "rstdd%d" % xr, "fng"], W=["yo%d" % xr])
                B.dma(ys[i][b * 256 + j * 128: b * 256 + (j + 1) * 128, :], st["yo"][xr], "yo%d" % xr,
                      R=["yo%d" % xr], W=[])

        load(0)
        for b in range(nblk):
            if b + 1 < nblk:
                load(b + 1)
            block(b)

    C1 = {}
    C2 = {}
    phaseA()
    for i in range(nseq):
        S.barrier()
        phaseB1(i)
    for i in range(nseq):
        S.barrier()
        phaseB2(i)
    S.barrier()
    for i in range(nseq):
        phaseC1(i, i == 0)
    S.barrier()
    for i in range(nseq):
        phaseC2(i, i == 0)
    S.emit(nc, es)
    es.close()
    return sc


def make_consts(TMAX):
    bf = ml_dtypes.bfloat16
    ident = np.eye(128, dtype=np.float32).astype(bf)
    ones = np.ones((128, 128), dtype=np.float32).astype(bf)
    R = np.zeros((128, 128), dtype=np.float32)
    for c in range(2):
        for j in range(8):
            R[c * 64 + j, c * 64 + j + 8] = -1.0
            R[c * 64 + 8 + j, c * 64 + j] = 1.0
    rt = np.ascontiguousarray(R.T).astype(bf)
    pos = np.arange(TMAX, dtype=np.float32)
    inv_freq = (np.float32(ROPE_THETA) ** (-np.arange(0, 16, 2, dtype=np.float32) / np.float32(16))).astype(np.float32)
    ang = (pos[None, :] * inv_freq[:, None]).astype(np.float32)
    cos = np.ones((128, TMAX), dtype=np.float32)
    sin = np.zeros((128, TMAX), dtype=np.float32)
    for c in range(2):
        for j in range(8):
            cos[c * 64 + j] = np.cos(ang[j])
            cos[c * 64 + 8 + j] = np.cos(ang[j])
            sin[c * 64 + j] = np.sin(ang[j])
            sin[c * 64 + 8 + j] = np.sin(ang[j])
    s = np.arange(64)[:, None]
    t = np.arange(64)[None, :]
    gm = np.stack([np.tile((t >= s).astype(np.float32), (1, 4)), np.tile((t <= s).astype(np.float32), (1, 4))])
    reset = np.zeros((128, 2, 512), dtype=np.float32)
    reset[:, 0, ::64] = 1.0
    reset[:, 1, 63::64] = 1.0
    return dict(c_ident=ident, c_rt=rt, c_ones=ones, c_inv32=np.full((64, 128), 1.0 / 32.0, dtype=np.float32), c_cos=cos, c_sin=sin, c_gmask=gm.astype(np.float32),
                c_reset=reset)


_CACHE = {}


def run(seq_inputs, weights, debug=False, core_ids=None):
    TS = tuple(int(a.shape[0]) for a in seq_inputs[0])
    nc = bass.Bass("TRN2", target_bir_lowering=False)
    build_program(nc, TS, debug)
    consts = make_consts(max(TS))
    f = np.float32
    common = {
        "norm1": weights["norm1"].reshape(D).astype(f),
        "w_in": np.ascontiguousarray(weights["w_in"].reshape(D, INW)).astype(f),
        "hg_lb_logits": weights["hg_lb_logits"].astype(f),
        "hg_norm": weights["hg_norm"].reshape(512).astype(f),
        "w_hg_branch": weights["w_hg_branch"].reshape(512, D).astype(f),
        "lq1": weights["da_lambda_q1"].reshape(64).astype(f),
        "lk1": weights["da_lambda_k1"].reshape(64).astype(f),
        "lq2": weights["da_lambda_q2"].reshape(64).astype(f),
        "lk2": weights["da_lambda_k2"].reshape(64).astype(f),
        "da_subln": weights["da_subln"].reshape(128).astype(f),
        "w_da_branch": weights["w_da_branch"].reshape(512, D).astype(f),
        "w_out": weights["w_out"].reshape(D, D).astype(f),
        "norm2": weights["norm2"].reshape(D).astype(f),
        "w_mlp_in": weights["w_mlp_in"].reshape(D, DFF).astype(f),
        "w_mlp_out": weights["w_mlp_out"].reshape(DFF, D).astype(f),
        "final_norm": weights["final_norm"].reshape(D).astype(f),
    }
    common.update(consts)
    in_maps = []
    for core in seq_inputs:
        m = dict(common)
        for i, a in enumerate(core):
            m["x%d" % i] = np.ascontiguousarray(a, dtype=f)
        in_maps.append(m)
    if core_ids is None:
        core_ids = list(range(len(in_maps)))
    res = run_bass_kernel_spmd(nc, in_maps, core_ids=core_ids)
    return res


def kernel(x_prompt, x_sample, norm1, w_in, hg_lb_logits, hg_norm, w_hg_branch,
           da_lambda_q1, da_lambda_k1, da_lambda_q2, da_lambda_k2, da_subln, w_da_branch,
           w_out, norm2, w_mlp_in, w_mlp_out, final_norm):
    weights = dict(norm1=np.asarray(norm1), w_in=np.asarray(w_in), hg_lb_logits=np.asarray(hg_lb_logits),
                   hg_norm=np.asarray(hg_norm), w_hg_branch=np.asarray(w_hg_branch),
                   da_lambda_q1=np.asarray(da_lambda_q1), da_lambda_k1=np.asarray(da_lambda_k1),
                   da_lambda_q2=np.asarray(da_lambda_q2), da_lambda_k2=np.asarray(da_lambda_k2),
                   da_subln=np.asarray(da_subln), w_da_branch=np.asarray(w_da_branch), w_out=np.asarray(w_out),
                   norm2=np.asarray(norm2), w_mlp_in=np.asarray(w_mlp_in), w_mlp_out=np.asarray(w_mlp_out),
                   final_norm=np.asarray(final_norm))
    xp = np.asarray(x_prompt)
    xsm = np.asarray(x_sample)
    seq_inputs = [[xp[c], xsm[c]] for c in range(NCORES)]
    res = run(seq_inputs, weights)
    yp = np.stack([res.results[c]["y0"] for c in range(NCORES)]).astype(np.float32)
    ysm = np.stack([res.results[c]["y1"] for c in range(NCORES)]).astype(np.float32)
    return (yp, ysm)
```

```python
import math
from contextlib import ExitStack

import numpy as np
import ml_dtypes
import concourse.bass as bass
import concourse.mybir as mybir
from concourse.bass_utils import run_bass_kernel_spmd

F32 = mybir.dt.float32
BF16 = mybir.dt.bfloat16
U8 = mybir.dt.uint8
AF = mybir.ActivationFunctionType
ALU = mybir.AluOpType

D = 1024
INW = 6144
DFF = 4096
HG_SCALE = 128 ** -0.5
DA_SCALE = 64 ** -0.5
NORM_EPS = 1e-6
SUBLN_EPS = 1e-5
LAM_INIT = 0.8 - 0.6 * math.exp(-0.3 * 0)
ROPE_THETA = 500000.0
NCORES = 8

COMPUTE = ("pe", "act", "dve", "pool")


class _Op:
    __slots__ = ("eng", "emit", "dma", "key", "pos", "waits", "signal", "sigval", "vcdone", "cum")


class Sched:
    def __init__(self):
        self.ops = []
        self.last_w = {}
        self.readers = {}
        self.npos = {e: 0 for e in COMPUTE + ("sp",)}
        self.keycount = {}
        self.extra = []
        self.rawdeps = []

    def add(self, eng, emit, R=(), W=(), key=None):
        op = _Op()
        op.eng = eng
        op.emit = emit
        op.dma = key is not None
        op.key = key
        op.pos = self.npos[eng]
        self.npos[eng] += 1
        op.waits = []
        op.signal = False
        op.sigval = None
        j = len(self.ops)
        deps = {}
        for i in self.extra:
            deps[i] = "BAR"
        for r in R:
            i = self.last_w.get(r)
            if i is not None:
                deps[i] = "RAW"
        for r in W:
            i = self.last_w.get(r)
            if i is not None and deps.get(i) != "RAW":
                deps.setdefault(i, "WAW")
            for i in self.readers.get(r, {}).values():
                if i != j:
                    deps.setdefault(i, "WAR")
        for r in W:
            self.last_w[r] = j
            self.readers[r] = {}
        for r in R:
            if r in W:
                continue
            k = ("k", key) if op.dma else eng
            self.readers.setdefault(r, {})[k] = j
        if op.dma:
            self.keycount[key] = self.keycount.get(key, 0) + 1
            op.cum = self.keycount[key]
        self.ops.append(op)
        self.rawdeps.append((deps, dict(self.keycount)))
        return j

    def barrier(self):
        last = {}
        for j, op in enumerate(self.ops):
            if op.dma:
                last[("k", op.key)] = j
            else:
                last[op.eng] = j
        self.extra = list(last.values())

    def resolve(self):
        vc = {e: {} for e in self.npos}
        for j, op in enumerate(self.ops):
            deps, kc = self.rawdeps[j]
            C = op.eng
            cur = vc[C]
            for i in sorted(deps):
                typ = deps[i]
                p = self.ops[i]
                if p.dma:
                    kk = "k:" + p.key
                    if op.dma and p.key == op.key:
                        if typ == "WAW":
                            continue
                        val = op.cum - 1
                    elif typ == "BAR":
                        val = p.cum
                    else:
                        val = kc.get(p.key, p.cum)
                    if cur.get(kk, 0) >= val:
                        continue
                    op.waits.append(("k", p.key, val))
                    cur[kk] = val
                    for a, b in p.vcdone.items():
                        if cur.get(a, -1) < b:
                            cur[a] = b
                else:
                    E = p.eng
                    if E == C and not op.dma and E == "pe":
                        continue
                    if cur.get(E, -1) >= p.pos:
                        continue
                    op.waits.append(("e", i))
                    p.signal = True
                    for a, b in p.vcdone.items():
                        if cur.get(a, -1) < b:
                            cur[a] = b
                    cur[E] = p.pos
            op.vcdone = dict(cur)
            if not op.dma:
                op.vcdone[C] = op.pos
        cnt = {e: 0 for e in COMPUTE}
        for op in self.ops:
            if not op.dma and op.signal:
                cnt[op.eng] += 1
                op.sigval = cnt[op.eng]

    def emit(self, nc, es):
        self.resolve()
        sems = {e: es.enter_context(nc.semaphore("s_" + e)) for e in COMPUTE}
        ksems = {k: es.enter_context(nc.semaphore("k_" + k)) for k in self.keycount}
        ops = self.ops

        def stream(engname, e):
            for op in ops:
                if op.eng != engname:
                    continue
                for w in op.waits:
                    if w[0] == "k":
                        e.wait_ge(ksems[w[1]], 16 * w[2])
                    else:
                        p = ops[w[1]]
                        e.wait_ge(sems[p.eng], p.sigval)
                ins = op.emit(e)
                if op.dma:
                    ins.then_inc(ksems[op.key], 16)
                elif op.signal:
                    ins.then_inc(sems[op.eng], 1)
            if engname == "sp":
                for k, c in self.keycount.items():
                    e.wait_ge(ksems[k], 16 * c)

        with nc.Block() as block:
            @block.sync
            def _(e):
                stream("sp", e)

            @block.tensor
            def _(e):
                stream("pe", e)

            @block.scalar
            def _(e):
                stream("act", e)

            @block.vector
            def _(e):
                stream("dve", e)

            @block.gpsimd
            def _(e):
                stream("pool", e)


class Builder:
    def __init__(self, nc, es, TS, debug=False):
        self.nc = nc
        self.es = es
        self.TS = TS
        self.S = Sched()
        self.debug = debug
        self.big = es.enter_context(nc.sbuf_tensor("big", [128, 206 * 1024], U8))
        self.psall = es.enter_context(nc.psum_tensor("psall", [128, 4096], F32))
        self.ps = [self.psall[:, i * 512:(i + 1) * 512] for i in range(8)]
        self.off = 0
        self.uid = 0

    def reset(self):
        self.off = 0

    def alloc(self, shape, dt):
        esz = 4 if dt == F32 else 2
        n = int(np.prod(shape[1:]))
        nb = (n * esz + 31) // 32 * 32
        assert self.off + nb <= 206 * 1024, ("SBUF overflow", self.off, nb)
        v = self.big[0:shape[0], self.off:self.off + n * esz].bitcast(dt)
        self.off += nb
        if len(shape) == 3:
            v = v.rearrange("p (a b) -> p a b", b=shape[2])
        elif len(shape) == 4:
            v = v.rearrange("p (a b c) -> p a b c", b=shape[2], c=shape[3])
        return v

    def name(self, s):
        self.uid += 1
        return "%s_%d" % (s, self.uid)

    def psb(self, i):
        return self.ps[i].bitcast(BF16)

    def dma(self, out, in_, key, R=(), W=(), **kw):
        self.S.add("sp", lambda e: e.dma_start(out=out, in_=in_, **kw), R, W, key=key)

    def mm(self, out, lhsT, rhs, start=True, stop=True, R=(), W=()):
        self.S.add("pe", lambda e: e.matmul(out, lhsT=lhsT, rhs=rhs, start=start, stop=stop), R, W)

    def tr(self, out, in_, ident, R=(), W=()):
        self.S.add("pe", lambda e: e.transpose(out, in_, ident), R, W)

    def act(self, out, in_, func, R=(), W=(), **kw):
        self.S.add("act", lambda e: e.activation(out=out, in_=in_, func=func, **kw), R, W)

    def dve(self, fn, R=(), W=()):
        self.S.add("dve", fn, R, W)

    def pool(self, fn, R=(), W=()):
        self.S.add("pool", fn, R, W)


def _tt(out, a, b, op):
    return lambda e: e.tensor_tensor(out=out, in0=a, in1=b, op=op)


def _ts(out, a, s1, s2, op0, op1=None):
    if op1 is None:
        return lambda e: e.tensor_scalar(out=out, in0=a, scalar1=s1, scalar2=None, op0=op0)
    return lambda e: e.tensor_scalar(out=out, in0=a, scalar1=s1, scalar2=s2, op0=op0, op1=op1)


def _stt(out, a, s, b, op0, op1):
    return lambda e: e.scalar_tensor_tensor(out=out, in0=a, scalar=s, in1=b, op0=op0, op1=op1)


def _cp(out, a):
    return lambda e: e.tensor_copy(out=out, in_=a)


def build_program(nc, TS, debug=False):
    es = ExitStack()
    B = Builder(nc, es, TS, debug)
    S = B.S
    nseq = len(TS)
    dk = "ExternalOutput" if debug else "Internal"

    def din(name, shape, dt=F32):
        return nc.dram_tensor(name, list(shape), dt, kind="ExternalInput").ap()

    xs = [din("x%d" % i, [T, D]) for i, T in enumerate(TS)]
    ys = [nc.dram_tensor("y%d" % i, [T, D], F32, kind="ExternalOutput").ap() for i, T in enumerate(TS)]
    norm1 = din("norm1", [D])
    w_in = din("w_in", [D, INW])
    lbl = din("hg_lb_logits", [2, 2, 512])
    hg_norm = din("hg_norm", [512])
    w_hg = din("w_hg_branch", [512, D])
    lq1 = din("lq1", [64]); lk1 = din("lk1", [64]); lq2 = din("lq2", [64]); lk2 = din("lk2", [64])
    subln = din("da_subln", [128])
    w_da = din("w_da_branch", [512, D])
    w_out = din("w_out", [D, D])
    norm2 = din("norm2", [D])
    w1 = din("w_mlp_in", [D, DFF])
    w2 = din("w_mlp_out", [DFF, D])
    fnorm = din("final_norm", [D])
    c_ident = din("c_ident", [128, 128], BF16)
    c_rt = din("c_rt", [128, 128], BF16)
    c_ones = din("c_ones", [128, 128], BF16)
    c_inv32 = din("c_inv32", [64, 128])
    TMAX = max(TS)
    c_cos = din("c_cos", [128, TMAX])
    c_sin = din("c_sin", [128, TMAX])
    c_gmask = din("c_gmask", [2, 64, 256])
    c_reset = din("c_reset", [128, 2, 512])

    def dscr(name, shape, dt):
        return nc.dram_tensor(name, list(shape), dt, kind=dk).ap()

    sc = []
    for i, T in enumerate(TS):
        d = {}
        d["qT"] = dscr("s%d_qT" % i, [4, 128, T], BF16)
        d["kT"] = dscr("s%d_kT" % i, [4, 128, T], BF16)
        d["vA"] = dscr("s%d_vA" % i, [4, 128, T // 128, 128], BF16)
        d["qg"] = dscr("s%d_qg" % i, [2, 4, 128, T], BF16)
        d["kg"] = dscr("s%d_kg" % i, [2, 4, 128, T], BF16)
        d["kt"] = dscr("s%d_kt" % i, [2, T, 512], BF16)
        d["vg"] = dscr("s%d_vg" % i, [T, 512], BF16)
        d["dd"] = dscr("s%d_dd" % i, [2, 128, 4, T // 64], F32)
        d["sg"] = dscr("s%d_sg" % i, [4, 128, T], BF16)
        d["gt"] = dscr("s%d_gt" % i, [16, 128, T], BF16)
        d["of"] = dscr("s%d_of" % i, [2, 4, 128, T], BF16)
        d["oa"] = dscr("s%d_oa" % i, [4, 128, T], BF16)
        d["x2"] = dscr("s%d_x2" % i, [T, D], F32)
        d["h2"] = dscr("s%d_h2" % i, [8, 128, T], BF16)
        sc.append(d)

    ps = B.ps
    RP = ["ps%d" % i for i in range(8)]

    def phaseA():
        B.reset()
        wA = B.alloc([128, 8, INW], BF16)
        xin = [B.alloc([128, 2, D], F32) for _ in range(2)]
        hT = [B.alloc([128, 8, 512], BF16) for _ in range(2)]
        xn = [B.alloc([128, D], BF16) for _ in range(2)]
        junk = B.alloc([128, D], BF16)
        cs = [B.alloc([128, 2, 512], F32) for _ in range(2)]
        sq = [B.alloc([128, 512], F32) for _ in range(4)]
        NR = 3
        ffb = [B.alloc([128, 512], F32) for _ in range(NR)]
        lb_ = [B.alloc([128, 512], F32) for _ in range(NR)]
        gb_ = [B.alloc([128, 512], F32) for _ in range(NR)]
        epb = [B.alloc([128, 512], F32) for _ in range(NR)]
        gsm = [B.alloc([128, 8], F32) for _ in range(NR)]
        kgb = [[B.alloc([128, 512], BF16) for _ in range(4)] for _ in range(2)]
        NST = 6
        stg = [B.alloc([128, 512], BF16) for _ in range(NST)]
        qb = [B.alloc([128, 512], BF16) for _ in range(2)]
        t1b = [B.alloc([128, 512], F32) for _ in range(3)]
        t2b = [B.alloc([128, 512], F32) for _ in range(2)]
        ident = B.alloc([128, 128], BF16)
        rt = B.alloc([128, 128], BF16)
        reset = B.alloc([128, 2, 512], F32)
        g1 = B.alloc([128, 8], F32)
        lgt = B.alloc([128, 2, 2, 4], F32)
        lbv = B.alloc([128, 8], F32)
        oml = B.alloc([128, 8], F32)
        ss = [B.alloc([128, 4], F32) for _ in range(2)]
        rstd = [B.alloc([128, 4], F32) for _ in range(2)]
        dbuf = [B.alloc([128, 2, 4, 8], F32) for _ in range(2)]
        print("phaseA sbuf bytes", B.off)

        B.dma(ident, c_ident, "cst", W=["ident"])
        B.dma(rt, c_rt, "cst", W=["rt"])
        B.dma(reset, c_reset, "cst", W=["reset"])
        B.dma(g1, norm1.rearrange("(c p) -> p c", p=128), "cst", W=["g1"], allow_slow_non_contiguous=True)
        for dr in range(2):
            for l in range(2):
                B.dma(lgt[:, dr, l, :], lbl[dr, l, :].rearrange("(h p) -> p h", p=128), "cst", W=["lgt"],
                      allow_slow_non_contiguous=True)
        for dr in range(2):
            B.dve(_tt(lbv[:, dr * 4:(dr + 1) * 4], lgt[:, dr, 0, :], lgt[:, dr, 1, :], ALU.subtract),
                  R=["lgt"], W=["lbv%d" % dr])
        B.act(lbv, lbv, AF.Sigmoid, R=["lbv0", "lbv1"], W=["lbv"])
        B.dve(_ts(oml, lbv, -1.0, 1.0, ALU.mult, ALU.add), R=["lbv"], W=["oml"])

        wi = 0
        for q in range(3):
            for kc in range(8):
                sl = wi % 2
                st = xin[sl].rearrange("p a b -> p (a b)")
                B.dma(st, w_in[kc * 128:(kc + 1) * 128, q * 2048:(q + 1) * 2048], "xin%d" % sl,
                      W=["xin%d" % sl])
                B.dve(_ts(wA[:, kc, q * 2048:(q + 1) * 2048], st, g1[:, kc:kc + 1], None, ALU.mult),
                      R=["xin%d" % sl, "g1"], W=["wA%d" % q])
                wi += 1

        blocks = [(i, b) for i, T in enumerate(TS) for b in range(T // 512)]
        cnt = {"st": 0, "mb": 0, "r": 0, "kb": 0}

        def load_block(n):
            i, b = blocks[n]
            sl = n % 2
            B.dma(cs[sl][:, 0, :], c_cos[:, b * 512:(b + 1) * 512], "cs%d" % sl, W=["cs%d" % sl])
            B.dma(cs[sl][:, 1, :], c_sin[:, b * 512:(b + 1) * 512], "cs%d" % sl, W=["cs%d" % sl])

        def load_x(n, half):
            i, b = blocks[n]
            sl = half
            B.dma(xin[sl], xs[i][b * 512 + half * 256: b * 512 + (half + 1) * 256, :].rearrange(
                "(j p) d -> p j d", p=128), "xin%d" % sl, W=["xin%d" % sl])

        def next_stage():
            k = cnt["st"] % NST
            cnt["st"] += 1
            return k

        def mainbank():
            k = 2 + cnt["mb"] % 3
            cnt["mb"] += 1
            return k

        def norm_stats(n):
            sl = n % 2
            for half in range(2):
                xv = xin[half]
                for jj in range(2):
                    j = half * 2 + jj
                    B.act(junk, xv[:, jj, :], AF.Square, R=["xin%d" % half], W=["junk", "ss%d_%d" % (sl, j)],
                          scale=1.0 / 32.0, accum_out=ss[sl][:, j:j + 1])
            B.act(rstd[sl], ss[sl], AF.Ln, R=["ss%d_%d" % (sl, j) for j in range(4)], W=["rstdl%d" % sl],
                  bias=NORM_EPS, scale=1.0)
            B.act(rstd[sl], rstd[sl], AF.Exp, R=["rstdl%d" % sl], W=["rstd%d" % sl], scale=-0.5)

        def norm_tile(n, j, defer):
            sl = n % 2
            half, jj = j // 2, j % 2
            xs_ = j % 2
            B.dve(_ts(xn[xs_], xin[half][:, jj, :], rstd[sl][:, j:j + 1], None, ALU.mult),
                  R=["xin%d" % half, "rstd%d" % sl], W=["xn%d" % xs_])

            def trans():
                tb = j % 2
                pv = B.psb(tb)
                for kc in range(8):
                    B.tr(pv[:, kc * 128:(kc + 1) * 128], xn[xs_][:, kc * 128:(kc + 1) * 128], ident,
                         R=["xn%d" % xs_, "ident"], W=[RP[tb]])

                def ev():
                    B.act(hT[sl][:, :, j * 128:(j + 1) * 128], pv.rearrange("p (a b) -> p a b", b=128), AF.Copy,
                          R=[], W=[RP[tb], "hT%d" % sl])
                if defer is None:
                    ev()
                else:
                    defer(2, ev)
            if defer is None:
                trans()
            else:
                defer(2, trans)
            if jj == 1 and n + 1 < len(blocks):
                load_x(n + 1, half)

        def chunk_mm(sl, fc, bank):
            for kc in range(8):
                B.mm(ps[bank], wA[:, kc, fc * 128:(fc + 1) * 128], hT[sl][:, kc, :],
                     start=(kc == 0), stop=(kc == 7), R=["wA%d" % (fc // 16), "hT%d" % sl], W=[RP[bank]])

        def tok_mm(sl, j, col0, bank):
            for kc in range(8):
                B.mm(ps[bank], hT[sl][:, kc, j * 128:(j + 1) * 128], wA[:, kc, col0:col0 + 512],
                     start=(kc == 0), stop=(kc == 7), R=["wA%d" % (col0 // 2048), "hT%d" % sl], W=[RP[bank]])

        def compute_block(n):
            i, b = blocks[n]
            sl = n % 2
            d = sc[i]
            tsl = slice(b * 512, (b + 1) * 512)
            deferred = []
            forcing = [False]

            def defer(delay, fn):
                if forcing[0]:
                    fn()
                else:
                    deferred.append([delay, fn])

            def flush(force=False):
                if force:
                    forcing[0] = True
                k = 0
                while k < len(deferred):
                    deferred[k][0] -= 1
                    if deferred[k][0] <= 0 or force:
                        deferred.pop(k)[1]()
                    else:
                        k += 1
                forcing[0] = False

            def t_qhg(h):
                bank = mainbank()
                chunk_mm(sl, 12 + h, bank)
                flush()
                r = cnt["r"] % NR
                cnt["r"] += 1
                B.act(lb_[r], ps[bank], AF.Sigmoid, R=[RP[bank]], W=["l%d" % r])

                def s1():
                    B.dve(_stt(sq[h], ps[bank], -HG_SCALE, lb_[r], ALU.mult, ALU.mult), R=["l%d" % r],
                          W=[RP[bank], "sq%d" % h])
                defer(1, s1)

            def t_f(dr, h):
                bank = mainbank()
                chunk_mm(sl, 16 + dr * 4 + h, bank)
                flush()
                r = cnt["r"] % NR
                cnt["r"] += 1
                col = dr * 4 + h
                B.act(ffb[r], ps[bank], AF.Sigmoid, R=[], W=[RP[bank], "ff%d" % r])

                def s1():
                    B.dve(_ts(ffb[r], ffb[r], oml[:, col:col + 1], lbv[:, col:col + 1], ALU.mult, ALU.add),
                          R=["oml", "lbv", "ff%d" % r], W=["ff%d" % r])

                def s2():
                    if dr == 0:
                        B.dve(lambda e: e.tensor_tensor_scan(out=epb[r], data0=reset[:, 0, :], data1=ffb[r],
                                                             initial=0.0, op0=ALU.max, op1=ALU.mult),
                              R=["ff%d" % r, "reset"], W=["ep%d" % r])
                    else:
                        B.dve(lambda e: e.tensor_tensor_scan(out=epb[r][:, ::-1], data0=reset[:, 1, ::-1],
                                                             data1=ffb[r][:, ::-1], initial=0.0,
                                                             op0=ALU.max, op1=ALU.mult),
                              R=["ff%d" % r, "reset"], W=["ep%d" % r])

                def s3():
                    B.dve(lambda e: e.reciprocal(out=gb_[r], in_=epb[r]), R=["ep%d" % r], W=["g%d" % r])
                    if dr == 0:
                        B.pool(_cp(dbuf[sl][:, dr, h, :], epb[r][:, 63::64]), R=["ep%d" % r], W=["dbuf%d" % sl])
                    else:
                        B.pool(_cp(dbuf[sl][:, dr, h, :], epb[r][:, 0::64]), R=["ep%d" % r], W=["dbuf%d" % sl])
                    k = next_stage()
                    B.pool(_tt(stg[k], sq[h], epb[r], ALU.mult), R=["sq%d" % h, "ep%d" % r], W=["stg%d" % k])
                    B.dma(d["qg"][dr, h, :, tsl], stg[k], "stg%d" % k, R=["stg%d" % k], W=[])

                def s4():
                    B.pool(_ts(ffb[r], ffb[r], 1.0, -1.0, ALU.mult, ALU.add), R=["ff%d" % r], W=["ff%d" % r])

                def s5():
                    B.pool(_tt(kgb[dr][h], ffb[r], gb_[r], ALU.mult), R=["ff%d" % r, "g%d" % r],
                           W=["kgb%d%d" % (dr, h)])
                    B.dma(d["kg"][dr, h, :, tsl], kgb[dr][h], "kgb%d%d" % (dr, h), R=["kgb%d%d" % (dr, h)], W=[])
                    if h == 3 and dr == 1:
                        for d2 in range(2):
                            B.dma(d["dd"][d2, :, :, b * 8:(b + 1) * 8], dbuf[sl][:, d2, :, :], "dbuf%d" % sl,
                                  R=["dbuf%d" % sl], W=[], allow_slow_non_contiguous=True)

                for k_, f_ in enumerate((s1, s2, s3, s4, s5)):
                    defer(k_ + 1, f_)
                if h == 3:
                    for j in range(4):
                        def ktrans(dr=dr, j=j):
                            pv = B.psb(7)
                            for hh in range(4):
                                B.tr(pv[:, hh * 128:(hh + 1) * 128], kgb[dr][hh][:, j * 128:(j + 1) * 128], ident,
                                     R=["kgb%d%d" % (dr, hh), "ident"], W=[RP[7]])

                            def kev():
                                k2 = next_stage()
                                B.dve(_cp(stg[k2], pv[:, 0:512]), R=[], W=[RP[7], "stg%d" % k2])
                                B.dma(d["kt"][dr, b * 512 + j * 128: b * 512 + (j + 1) * 128, :], stg[k2],
                                      "stg%d" % k2, R=["stg%d" % k2], W=[])
                            defer(2, kev)
                        defer(7 + 2 * j, ktrans)

            def t_qk(fc):
                bank = mainbank()
                chunk_mm(sl, fc, bank)
                flush()
                r = fc % 2
                r3 = fc % 3
                rb = 5 + fc % 2
                B.act(qb[r], ps[bank], AF.Copy, R=[], W=[RP[bank], "qb%d" % r])

                def s1():
                    B.mm(ps[rb], rt, qb[r], R=["rt", "qb%d" % r], W=[RP[rb]])
                    B.pool(_tt(t1b[r3], qb[r], cs[sl][:, 0, :], ALU.mult), R=["qb%d" % r, "cs%d" % sl],
                           W=["t1%d" % r3])

                def s2():
                    B.dve(_tt(t2b[r], ps[rb], cs[sl][:, 1, :], ALU.mult), R=["cs%d" % sl], W=[RP[rb], "t2%d" % r])

                def s3():
                    k = next_stage()
                    B.pool(_tt(stg[k], t1b[r3], t2b[r], ALU.add), R=["t1%d" % r3, "t2%d" % r], W=["stg%d" % k])
                    dst = d["qT"] if fc < 4 else d["kT"]
                    B.dma(dst[fc % 4, :, tsl], stg[k], "stg%d" % k, R=["stg%d" % k], W=[])
                for k_, f_ in enumerate((s1, s2, s3)):
                    defer(k_ + 1, f_)

            def t_v(j):
                bank = mainbank()
                tok_mm(sl, j, 1024, bank)
                flush()
                k = next_stage()
                B.act(stg[k], ps[bank], AF.Copy, R=[], W=[RP[bank], "stg%d" % k])
                B.dma(d["vA"][:, :, b * 4 + j, :].rearrange("h p v -> p h v"),
                      stg[k].rearrange("p (h v) -> p h v", v=128), "stg%d" % k, R=["stg%d" % k], W=[])

            def t_i(j):
                bank = mainbank()
                tok_mm(sl, j, 3072, bank)
                flush()
                k = next_stage()
                B.act(stg[k], ps[bank], AF.Copy, R=[], W=[RP[bank], "stg%d" % k])
                B.dma(d["vg"][b * 512 + j * 128: b * 512 + (j + 1) * 128, :], stg[k], "stg%d" % k,
                      R=["stg%d" % k], W=[])

            def t_ghg(h):
                bank = mainbank()
                chunk_mm(sl, 28 + h, bank)
                flush()
                r = cnt["r"] % NR
                cnt["r"] += 1
                B.act(lb_[r], ps[bank], AF.Sigmoid, R=[RP[bank]], W=["l%d" % r])

                def s1():
                    k = next_stage()
                    B.dve(_tt(stg[k], ps[bank], lb_[r], ALU.mult), R=["l%d" % r], W=[RP[bank], "stg%d" % k])
                    B.dma(d["sg"][h, :, tsl], stg[k], "stg%d" % k, R=["stg%d" % k], W=[])
                defer(1, s1)

            def t_gate(c):
                bank = mainbank()
                chunk_mm(sl, 32 + c, bank)
                flush()
                k = next_stage()
                B.act(stg[k], ps[bank], AF.Sigmoid, R=[], W=[RP[bank], "stg%d" % k])
                B.dma(d["gt"][c, :, tsl], stg[k], "stg%d" % k, R=["stg%d" % k], W=[])

            others = ([(t_ghg, (h,)) for h in range(4)] + [(t_gate, (c,)) for c in range(16)]
                      + [(t_qk, (fc,)) for fc in range(8)] + [(t_v, (j,)) for j in range(4)]
                      + [(t_i, (j,)) for j in range(4)])
            order = [(t_qhg, (h,)) for h in range(4)]
            oi = 0
            for dr in range(2):
                for h in range(4):
                    order.append((t_f, (dr, h)))
                    order += others[oi:oi + 4]
                    oi += 4
            order += others[oi:]
            have_next = n + 1 < len(blocks)
            if have_next:
                load_block(n + 1)
                norm_stats(n + 1)
            for pos, (fn, args) in enumerate(order):
                fn(*args)
                if have_next and pos in (20, 24, 28, 32):
                    norm_tile(n + 1, (pos - 20) // 4, defer)
            flush(force=True)

        load_block(0)
        load_x(0, 0)
        load_x(0, 1)
        norm_stats(0)
        for j in range(4):
            norm_tile(0, j, None)
        for n in range(len(blocks)):
            compute_block(n)

    def phaseB1(i):
        B.reset()
        T = TS[i]
        d = sc[i]
        nblk = T // 512
        nch = T // 64
        Sst = [B.alloc([128, 512], F32) for _ in range(2)]
        Sb = [B.alloc([128, 512], BF16) for _ in range(2)]
        T1 = [B.alloc([128, 512], F32) for _ in range(2)]
        Ab = [B.alloc([64, 256], BF16) for _ in range(2)]
        gmask = B.alloc([64, 2, 256], F32)
        qg = [[B.alloc([128, 4, 512], BF16) for _ in range(2)] for _ in range(2)]
        kg = [[B.alloc([128, 4, 512], BF16) for _ in range(2)] for _ in range(2)]
        kt = [[B.alloc([64, 8, 512], BF16) for _ in range(2)] for _ in range(2)]
        vg = [[B.alloc([64, 8, 512], BF16) for _ in range(2)] for _ in range(2)]
        dd = [[B.alloc([128, 4, 8], F32) for _ in range(2)] for _ in range(2)]
        ob = [[B.alloc([128, 4, 512], BF16) for _ in range(2)] for _ in range(2)]
        for dr in range(2):
            B.dma(gmask[:, dr, :], c_gmask[dr], "cst", W=["gmask"])

        def load(dr, bi):
            blk = bi if dr == 0 else nblk - 1 - bi
            sl = bi % 2
            tsl = slice(blk * 512, (blk + 1) * 512)
            nm = "b1in%d%d" % (dr, sl)
            B.dma(qg[dr][sl], d["qg"][dr, :, :, tsl].rearrange("h p t -> p h t"), nm, W=[nm])
            B.dma(kg[dr][sl], d["kg"][dr, :, :, tsl].rearrange("h p t -> p h t"), nm, W=[nm])
            B.dma(kt[dr][sl], d["kt"][dr, tsl, :].rearrange("(c p) f -> p c f", p=64), nm, W=[nm])
            B.dma(vg[dr][sl], d["vg"][tsl, :].rearrange("(c p) f -> p c f", p=64), nm, W=[nm])
            B.dma(dd[dr][sl], d["dd"][dr, :, :, blk * 8:(blk + 1) * 8], nm, W=[nm], allow_slow_non_contiguous=True)

        def idx(dr, st):
            bi = st // 8
            ci = st % 8
            blk = bi if dr == 0 else nblk - 1 - bi
            cc = ci if dr == 0 else 7 - ci
            return bi, ci, blk, cc, bi % 2

        def emit_A(dr, st):
            bi, ci, blk, cc, sl = idx(dr, st)
            nm = "b1in%d%d" % (dr, sl)
            pA, psA = RP[dr * 3], ps[dr * 3]
            csl = slice(cc * 64, (cc + 1) * 64)
            for h in range(4):
                B.mm(psA[0:64, h * 64:(h + 1) * 64], kg[dr][sl][:, h, csl], qg[dr][sl][:, h, csl],
                     R=[nm], W=[pA])
            B.dve(_tt(Ab[dr], psA[0:64, 0:256], gmask[:, dr, :], ALU.mult), R=["gmask"], W=[pA, "Ab%d" % dr])

        def emit_main(dr, st):
            bi, ci, blk, cc, sl = idx(dr, st)
            nm = "b1in%d%d" % (dr, sl)
            first = (st == 0)
            pO, pP = RP[dr * 3 + 1], RP[dr * 3 + 2]
            psO, psP = ps[dr * 3 + 1], ps[dr * 3 + 2]
            csl = slice(cc * 64, (cc + 1) * 64)
            for h in range(4):
                B.mm(psP[:, h * 128:(h + 1) * 128], kt[dr][sl][:, cc, h * 128:(h + 1) * 128],
                     vg[dr][sl][:, cc, h * 128:(h + 1) * 128], R=[nm], W=[pP])
            if first:
                B.dve(_cp(T1[dr], psP), R=[], W=[pP, "T1%d" % dr])
            else:
                B.dve(_tt(T1[dr], psP, Sst[dr], ALU.add), R=["S%d" % dr], W=[pP, "T1%d" % dr])
            for h in range(4):
                B.mm(psO[:, h * 64:(h + 1) * 64], vg[dr][sl][:, cc, h * 128:(h + 1) * 128],
                     Ab[dr][:, h * 64:(h + 1) * 64], start=True, stop=first, R=[nm, "Ab%d" % dr], W=[pO])
                if not first:
                    sbn = ("Sba%d" % dr) if h < 2 else ("Sbb%d_%d" % (dr, h))
                    B.mm(psO[:, h * 64:(h + 1) * 64], Sb[dr][:, h * 128:(h + 1) * 128], qg[dr][sl][:, h, csl],
                         start=False, stop=True, R=[nm, sbn], W=[pO])
            onm = "ob%d%d" % (dr, sl)
            dbc = dd[dr][sl][:, :, cc:cc + 1].broadcast_to([128, 4, 128])
            dbc2 = dd[dr][sl][:, 0:2, cc:cc + 1].broadcast_to([128, 2, 128])
            B.dve(_tt(Sb[dr][:, 0:256].rearrange("p (h v) -> p h v", v=128),
                      T1[dr][:, 0:256].rearrange("p (h v) -> p h v", v=128), dbc2, ALU.mult),
                  R=["T1%d" % dr, nm], W=["Sba%d" % dr])
            for h in (2, 3):
                B.act(Sb[dr][:, h * 128:(h + 1) * 128], T1[dr][:, h * 128:(h + 1) * 128], AF.Copy,
                      R=["T1%d" % dr, nm], W=["Sbb%d_%d" % (dr, h)], scale=dd[dr][sl][:, h, cc:cc + 1])
            B.act(ob[dr][sl][:, :, csl], psO[:, 0:256].rearrange("p (h t) -> p h t", t=64), AF.Copy,
                  R=[], W=[pO, onm])
            B.pool(_tt(Sst[dr].rearrange("p (h v) -> p h v", v=128), T1[dr].rearrange("p (h v) -> p h v", v=128),
                       dbc, ALU.mult), R=["T1%d" % dr, nm], W=["S%d" % dr])
            if ci == 7:
                tsl = slice(blk * 512, (blk + 1) * 512)
                B.dma(d["of"][dr, :, :, tsl].rearrange("h p t -> p h t"), ob[dr][sl], onm, R=[onm], W=[])

        for dr in range(2):
            load(dr, 0)
        for dr in range(2):
            emit_A(dr, 0)
        for st in range(nch):
            if st % 8 == 0 and st // 8 + 1 < nblk:
                for dr in range(2):
                    load(dr, st // 8 + 1)
            for dr in range(2):
                emit_main(dr, st)
            if st + 1 < nch:
                for dr in range(2):
                    emit_A(dr, st + 1)

    def phaseB2(i):
        B.reset()
        T = TS[i]
        d = sc[i]
        NKT = T // 128
        NQB = T // 512
        KT = [B.alloc([128, T], BF16) for _ in range(2)]
        QT = [B.alloc([128, T], BF16) for _ in range(2)]
        VV = [B.alloc([128, NKT, 128], BF16) for _ in range(2)]
        NP = 4
        pT2 = [B.alloc([128, 1024], BF16) for _ in range(NP)]
        pp2 = [B.alloc([128, 1024], BF16) for _ in range(2)]
        lrows = B.alloc([64, 512], F32)
        ones = B.alloc([128, 128], BF16)
        inv32 = B.alloc([64, 128], F32)
        osb = [B.alloc([128, 512], F32) for _ in range(2)]
        on = [B.alloc([128, 512], F32) for _ in range(2)]
        diff = B.alloc([128, 512], F32)
        sqd = B.alloc([128, 512], BF16)
        lnv = B.alloc([128, 512], F32)
        ost = [B.alloc([128, 512], BF16) for _ in range(2)]
        lv = B.alloc([128, 4, 64], F32)
        lt = B.alloc([128, 2, 64], F32)
        ls = B.alloc([128, 2], F32)
        neglam = B.alloc([128, 1], F32)
        gsub = B.alloc([128, 1], F32)
        B.dma(ones, c_ones, "cst", W=["ones"])
        B.dma(inv32, c_inv32, "cst", W=["inv32"])
        for k, v in enumerate([lq1, lk1, lq2, lk2]):
            B.dma(lv[:, k, :], v.partition_broadcast(128), "cst", W=["lv"])
        B.dma(gsub, subln.rearrange("(p o) -> p o", o=1), "cst", W=["gsub0"])
        B.dve(_ts(gsub, gsub, 1.0 - LAM_INIT, None, ALU.mult), R=["gsub0"], W=["gsub"])
        for k in range(2):
            B.dve(_tt(lt[:, k, :], lv[:, 2 * k, :], lv[:, 2 * k + 1, :], ALU.mult), R=["lv"], W=["lt%d" % k])
            B.dve(lambda e, k=k: e.reduce_sum(out=ls[:, k:k + 1], in_=lt[:, k, :], axis=mybir.AxisListType.X),
                  R=["lt%d" % k], W=["ls%d" % k])
        B.act(ls, ls, AF.Exp, R=["ls0", "ls1"], W=["lse"])
        B.dve(_tt(neglam, ls[:, 1:2], ls[:, 0:1], ALU.subtract), R=["lse"], W=["nl0"])
        B.dve(_ts(neglam, neglam, -LAM_INIT, None, ALU.add), R=["nl0"], W=["neglam"])

        def load_head(h):
            sl = h % 2
            nm = "hd%d" % sl
            B.dma(KT[sl], d["kT"][h], nm, W=[nm])
            B.dma(QT[sl], d["qT"][h], nm, W=[nm])
            for g in range(0, NKT, 16):
                ge = min(NKT, g + 16)
                B.dma(VV[sl][:, g:ge, :], d["vA"][h, :, g:ge, :], nm, W=[nm])

        cnt = {"p": 0, "s": 0, "o": 0, "q": 0}
        prev = [0]
        psall = B.psall

        def head(h):
            sl = h % 2
            nm = "hd%d" % sl
            fin = []
            for qbk in range(NQB):
                qsl = slice(qbk * 512, (qbk + 1) * 512)
                pend = []
                sumq = []

                def qk(kt_):
                    g = cnt["s"] % 2
                    cnt["s"] += 1
                    b0, b1 = 2 * g, 2 * g + 1
                    ksl = slice(kt_ * 128, (kt_ + 1) * 128)
                    B.mm(ps[b0], KT[sl][0:64, ksl], QT[sl][0:64, qsl], R=[nm], W=[RP[b0]])
                    B.mm(ps[b1], KT[sl][64:128, ksl], QT[sl][64:128, qsl], R=[nm], W=[RP[b1]])
                    pi = cnt["p"] % NP
                    cnt["p"] += 1
                    B.act(pT2[pi], psall[:, b0 * 512:(b0 + 2) * 512], AF.Exp, R=[], W=[RP[b0], RP[b1], "pT%d" % pi],
                          scale=DA_SCALE)
                    pend.append((kt_, pi))

                def sums():
                    q2, kt_ = sumq.pop(0)
                    for c in range(2):
                        B.mm(ps[6][32 * c:32 * c + 32, :], ones[:, 0:32], pp2[q2][:, c * 512:(c + 1) * 512],
                             start=(kt_ == 1), stop=(kt_ == NKT - 1), R=["ones", "pp%d" % q2], W=[RP[6]])

                def pv():
                    kt_, pi = pend.pop(0)
                    for c in range(2):
                        B.mm(ps[4 + c], VV[sl][:, kt_, :], pT2[pi][:, c * 512:(c + 1) * 512], start=(kt_ == 0),
                             stop=(kt_ == NKT - 1), R=[nm, "pT%d" % pi], W=[RP[4 + c]])
                    if sumq:
                        sums()
                    if kt_ % 2 == 0:
                        prev[0] = pi
                    else:
                        q2 = cnt["q"] % 2
                        cnt["q"] += 1
                        B.dve(_tt(pp2[q2], pT2[prev[0]], pT2[pi], ALU.add), R=["pT%d" % prev[0], "pT%d" % pi],
                              W=["pp%d" % q2])
                        sumq.append((q2, kt_))
                    if fin:
                        fin.pop(0)()

                for kt_ in range(NKT):
                    qk(kt_)
                    if len(pend) > 2:
                        pv()
                while pend:
                    pv()
                while sumq:
                    sums()
                while fin:
                    fin.pop(0)()
                for c in range(2):
                    B.dve(_cp(osb[c], ps[4 + c]), R=[], W=[RP[4 + c], "osb%d" % c])
                B.dve(_cp(lrows, ps[6][0:64, :]), R=[], W=[RP[6], "lrows"])

                def f1():
                    B.dve(lambda e: e.reciprocal(out=lrows, in_=lrows), R=["lrows"], W=["lrows"])

                def f2(c):
                    def f():
                        B.mm(ps[7], inv32[32 * c:32 * c + 32, :], lrows[32 * c:32 * c + 32, :], R=["inv32", "lrows"],
                             W=[RP[7]])
                        B.dve(_tt(on[c], osb[c], ps[7], ALU.mult), R=["osb%d" % c], W=[RP[7], "on%d" % c])
                    return f

                def f4():
                    B.dve(_stt(diff, on[1], neglam[:, 0:1], on[0], ALU.mult, ALU.add), R=["on0", "on1", "neglam"],
                          W=["diff"])
                    B.dve(_tt(sqd, diff, diff, ALU.mult), R=["diff"], W=["sqd"])

                def f5():
                    B.mm(ps[7], ones, sqd, R=["ones", "sqd"], W=[RP[7]])

                def f6():
                    B.act(lnv, ps[7], AF.Ln, R=[], W=[RP[7], "lnv"], scale=1.0 / 128.0, bias=SUBLN_EPS)
                    B.act(lnv, lnv, AF.Exp, R=["lnv"], W=["rsd"], scale=-0.5)

                def f7(qsl=qsl):
                    k = cnt["o"] % 2
                    cnt["o"] += 1
                    B.dve(_stt(ost[k], diff, gsub[:, 0:1], lnv, ALU.mult, ALU.mult), R=["diff", "gsub", "rsd"],
                          W=["ost%d" % k])
                    B.dma(d["oa"][h, :, qsl], ost[k], "ost%d" % k, R=["ost%d" % k], W=[])
                fin.extend([f1, f2(0), f2(1), f4, f5, f6, f7])
            while fin:
                fin.pop(0)()

        load_head(0)
        for h in range(4):
            if h + 1 < 4:
                load_head(h + 1)
            head(h)

    def phaseC1(i, first):
        T = TS[i]
        d = sc[i]
        st = C1
        if first:
            B.reset()
            st["whg"] = B.alloc([128, 4, D], BF16)
            st["wda"] = B.alloc([128, 4, D], BF16)
            st["wo"] = B.alloc([128, 8, D], BF16)
            st["of"] = [B.alloc([128, 2, 4, 512], BF16) for _ in range(2)]
            st["sg"] = [B.alloc([128, 4, 512], BF16) for _ in range(2)]
            st["oa"] = [B.alloc([128, 4, 512], BF16) for _ in range(2)]
            st["gt"] = [B.alloc([128, 16, 512], BF16) for _ in range(2)]
            st["x"] = [B.alloc([128, 4, D], F32) for _ in range(2)]
            st["osum"] = [B.alloc([128, 512], F32) for _ in range(2)]
            st["sqo"] = [B.alloc([128, 512], BF16) for _ in range(2)]
            st["rs"] = [B.alloc([128, 512], F32) for _ in range(2)]
            st["ohg"] = [B.alloc([128, 4, 512], BF16) for _ in range(2)]
            st["mT"] = B.alloc([128, 8, 512], BF16)
            st["t1"] = [B.alloc([128, 512], F32) for _ in range(2)]
            st["t2"] = [B.alloc([128, 512], F32) for _ in range(2)]
            st["x2s"] = [B.alloc([128, D], F32) for _ in range(3)]
            st["xn2"] = [B.alloc([128, D], BF16) for _ in range(2)]
            st["h2s"] = [B.alloc([128, 8, 128], BF16) for _ in range(2)]
            st["junk"] = B.alloc([128, D], BF16)
            st["ones"] = B.alloc([128, 128], BF16)
            st["ident"] = B.alloc([128, 128], BF16)
            st["ghg"] = B.alloc([128, 4], F32)
            st["ss"] = B.alloc([128, 4], F32)
            st["rstd"] = B.alloc([128, 4], F32)
            st["cnt"] = {"t": 0, "x": 0, "o": 0}
            st["deferred"] = []
            print("phaseC1 sbuf bytes", B.off)
            B.dma(st["ones"], c_ones, "cst", W=["ones"])
            B.dma(st["ident"], c_ident, "cst", W=["ident"])
            B.dma(st["ghg"], hg_norm.rearrange("(h p) -> p h", p=128), "cst", W=["ghg"], allow_slow_non_contiguous=True)
            wi = 0
            for (wsrc, wdst, nk) in ((w_hg, st["whg"], 4), (w_da, st["wda"], 4), (w_out, st["wo"], 8)):
                for kc in range(0, nk, 4):
                    sl = wi % 2
                    B.dma(st["x"][sl], wsrc[kc * 128:(kc + 4) * 128, :].rearrange("(c p) d -> p c d", p=128),
                          "c1in%d" % sl, W=["c1in%d" % sl])
                    B.dve(_cp(wdst[:, kc:kc + 4, :], st["x"][sl]), R=["c1in%d" % sl], W=["wC1"])
                    wi += 1
        whg, wda, wo = st["whg"], st["wda"], st["wo"]
        ones, ident, ghg = st["ones"], st["ident"], st["ghg"]
        nblk = T // 512
        cnt = st["cnt"]
        deferred = st["deferred"]

        def defer(delay, fn):
            deferred.append([delay, fn])

        def flush(force=False):
            k = 0
            while k < len(deferred):
                deferred[k][0] -= 1
                if deferred[k][0] <= 0 or force:
                    deferred.pop(k)[1]()
                else:
                    k += 1

        def load(b):
            sl = b % 2
            nm = "c1in%d" % sl
            tsl = slice(b * 512, (b + 1) * 512)
            for dr in range(2):
                B.dma(st["of"][sl][:, dr, :, :], d["of"][dr, :, :, tsl].rearrange("h p t -> p h t"), nm, W=[nm])
            B.dma(st["sg"][sl], d["sg"][:, :, tsl].rearrange("h p t -> p h t"), nm, W=[nm])
            B.dma(st["oa"][sl], d["oa"][:, :, tsl].rearrange("h p t -> p h t"), nm, W=[nm])
            B.dma(st["gt"][sl], d["gt"][:, :, tsl].rearrange("h p t -> p h t"), nm, W=[nm])
            B.dma(st["x"][sl], xs[i][tsl, :].rearrange("(j p) d -> p j d", p=128), nm, W=[nm])

        def prologue_head(b, h):
            sl = b % 2
            nm = "c1in%d" % sl
            r = cnt["o"] % 2
            cnt["o"] += 1
            osum, sqo, rs = st["osum"][r], st["sqo"][r], st["rs"][r]
            ohg = st["ohg"][sl]
            B.pool(_tt(osum, st["of"][sl][:, 0, h, :], st["of"][sl][:, 1, h, :], ALU.add), R=[nm], W=["osum%d" % r])
            B.act(sqo, osum, AF.Square, R=["osum%d" % r], W=["sqo%d" % r])

            pb = 7 if h % 2 == 0 else 3

            def rest():
                B.mm(ps[pb], ones, sqo, R=["ones", "sqo%d" % r], W=[RP[pb]])
                B.act(rs, ps[pb], AF.Ln, R=[], W=[RP[pb], "rsl%d" % r], scale=1.0 / 128.0, bias=NORM_EPS)
                B.act(rs, rs, AF.Exp, R=["rsl%d" % r], W=["rs%d" % r], scale=-0.5)
                B.dve(_tt(osum, osum, rs, ALU.mult), R=["rs%d" % r, "osum%d" % r], W=["osum%d" % r])
                B.dve(_stt(ohg[:, h, :], osum, ghg[:, h:h + 1], st["sg"][sl][:, h, :], ALU.mult, ALU.mult),
                      R=["osum%d" % r, "ghg", nm], W=["ohg%d" % sl])
            return rest

        def proj_chunk(b, dc):
            sl = b % 2
            nm = "c1in%d" % sl
            mT = st["mT"]
            r = cnt["t"] % 2
            cnt["t"] += 1
            ba, bb = 0 + 2 * r, 1 + 2 * r
            for kc in range(4):
                B.mm(ps[ba], whg[:, kc, dc * 128:(dc + 1) * 128], st["ohg"][sl][:, kc, :], start=(kc == 0),
                     stop=(kc == 3), R=["wC1", "ohg%d" % sl], W=[RP[ba]])
            for kc in range(4):
                B.mm(ps[bb], wda[:, kc, dc * 128:(dc + 1) * 128], st["oa"][sl][:, kc, :], start=(kc == 0),
                     stop=(kc == 3), R=["wC1", nm], W=[RP[bb]])
            B.dve(_tt(st["t1"][r], ps[ba], st["gt"][sl][:, dc, :], ALU.mult), R=[nm], W=[RP[ba], "t1%d" % r])
            B.dve(_tt(st["t2"][r], ps[bb], st["gt"][sl][:, 8 + dc, :], ALU.mult), R=[nm], W=[RP[bb], "t2%d" % r])
            B.pool(_tt(mT[:, dc, :], st["t1"][r], st["t2"][r], ALU.add), R=["t1%d" % r, "t2%d" % r], W=["mT"])

        def out_tile(b, j):
            sl = b % 2
            nm = "c1in%d" % sl
            mT = st["mT"]
            xr = cnt["x"] % 3
            hr = cnt["x"] % 2
            sr = cnt["x"] % 4
            cnt["x"] += 1
            x2s = st["x2s"][xr]
            for half in range(2):
                bank = 4 + half
                for kc in range(8):
                    B.mm(ps[bank], mT[:, kc, j * 128:(j + 1) * 128], wo[:, kc, half * 512:(half + 1) * 512],
                         start=(kc == 0), stop=(kc == 7), R=["mT", "wC1"], W=[RP[bank]])
                B.dve(_tt(x2s[:, half * 512:(half + 1) * 512], ps[bank],
                          st["x"][sl][:, j, half * 512:(half + 1) * 512], ALU.add), R=[nm],
                      W=[RP[bank], "x2s%d" % xr])
            B.dma(d["x2"][b * 512 + j * 128: b * 512 + (j + 1) * 128, :], x2s, "x2s%d" % xr, R=["x2s%d" % xr], W=[])
            B.act(st["junk"], x2s, AF.Square, R=["x2s%d" % xr], W=["junk", "ssc%d" % sr], scale=1.0 / 32.0,
                  accum_out=st["ss"][:, sr:sr + 1])
            B.act(st["rstd"][:, sr:sr + 1], st["ss"][:, sr:sr + 1], AF.Ln, R=["ssc%d" % sr], W=["rsc%d" % sr],
                  bias=NORM_EPS, scale=1.0)
            B.act(st["rstd"][:, sr:sr + 1], st["rstd"][:, sr:sr + 1], AF.Exp, R=["rsc%d" % sr], W=["rstdc%d" % sr],
                  scale=-0.5)
            B.dve(_ts(st["xn2"][hr], x2s, st["rstd"][:, sr:sr + 1], None, ALU.mult),
                  R=["x2s%d" % xr, "rstdc%d" % sr], W=["xn2%d" % hr])

            def trans():
                pv = B.psb(6)
                for kc in range(8):
                    B.tr(pv[:, kc * 128:(kc + 1) * 128], st["xn2"][hr][:, kc * 128:(kc + 1) * 128], ident,
                         R=["xn2%d" % hr, "ident"], W=[RP[6]])
                B.act(st["h2s"][hr], pv.rearrange("p (a b) -> p a b", b=128), AF.Copy, R=[], W=[RP[6], "h2s%d" % hr])
                B.dma(d["h2"][:, :, b * 512 + j * 128: b * 512 + (j + 1) * 128].rearrange("c p t -> p c t"),
                      st["h2s"][hr], "h2s%d" % hr, R=["h2s%d" % hr], W=[])
            defer(2, trans)

        load(0)
        for h in range(4):
            prologue_head(0, h)()
        for b in range(nblk):
            nxt = b + 1 < nblk
            if nxt:
                load(b + 1)
            for dc in range(8):
                proj_chunk(b, dc)
                flush()
            rests = {}
            out_tile(b, 0)
            flush()
            if nxt:
                rests[0] = prologue_head(b + 1, 0)
                rests[1] = prologue_head(b + 1, 1)
            out_tile(b, 1)
            flush()
            if nxt:
                rests[0]()
                rests[1]()
                rests[2] = prologue_head(b + 1, 2)
                rests[3] = prologue_head(b + 1, 3)
            out_tile(b, 2)
            flush()
            if nxt:
                rests[2]()
                rests[3]()
            out_tile(b, 3)
            flush()
        flush(force=True)

    def phaseC2(i, first):
        T = TS[i]
        d = sc[i]
        st = C2
        if first:
            B.reset()
            st["w1"] = B.alloc([128, 8, DFF], BF16)
            st["w2"] = B.alloc([128, 32, D], BF16)
            st["h2"] = [B.alloc([128, 8, 256], BF16) for _ in range(2)]
            st["x2"] = [B.alloc([128, 2, D], F32) for _ in range(2)]
            st["rl"] = [B.alloc([128, 256], F32) for _ in range(3)]
            st["hid"] = [B.alloc([128, 256], BF16) for _ in range(6)]
            st["x3"] = [B.alloc([128, D], F32) for _ in range(2)]
            st["yo"] = [B.alloc([128, D], F32) for _ in range(2)]
            st["fng"] = B.alloc([128, D], F32)
            st["junk"] = B.alloc([128, D], BF16)
            st["g2"] = B.alloc([128, 8], F32)
            st["ss"] = B.alloc([128, 2], F32)
            st["rstd"] = B.alloc([128, 2], F32)
            B.dma(st["fng"], fnorm.partition_broadcast(128), "cst", W=["fng"])
            B.dma(st["g2"], norm2.rearrange("(c p) -> p c", p=128), "cst", W=["g2"], allow_slow_non_contiguous=True)
            wi = 0
            for q in range(2):
                for kc in range(8):
                    sl = wi % 2
                    stv = st["x2"][sl].rearrange("p a b -> p (a b)")
                    B.dma(stv, w1[kc * 128:(kc + 1) * 128, q * 2048:(q + 1) * 2048], "c2in%d" % sl, W=["c2in%d" % sl])
                    B.dve(_ts(st["w1"][:, kc, q * 2048:(q + 1) * 2048], stv, st["g2"][:, kc:kc + 1], None, ALU.mult),
                          R=["c2in%d" % sl, "g2"], W=["w1C2_%d" % q])
                    wi += 1
            for fc in range(0, 32, 2):
                sl = wi % 2
                B.dma(st["x2"][sl], w2[fc * 128:(fc + 2) * 128, :].rearrange("(c p) d -> p c d", p=128),
                      "c2in%d" % sl, W=["c2in%d" % sl])
                B.act(st["w2"][:, fc:fc + 2, :], st["x2"][sl], AF.Copy, R=["c2in%d" % sl], W=["w2C2_%d" % (fc // 2)])
                wi += 1
        W1, W2 = st["w1"], st["w2"]
        nblk = T // 256

        def load(b):
            sl = b % 2
            nm = "c2in%d" % sl
            tsl = slice(b * 256, (b + 1) * 256)
            B.dma(st["h2"][sl], d["h2"][:, :, tsl].rearrange("c p t -> p c t"), "c2h%d" % sl, W=["c2h%d" % sl])
            B.dma(st["x2"][sl], d["x2"][tsl, :].rearrange("(j p) d -> p j d", p=128), nm, W=[nm])

        cnt = {"h": 0, "r": 0, "y": 0}

        def block(b):
            sl = b % 2
            nm = "c2in%d" % sl
            hn = "c2h%d" % sl
            pend = []

            def l2():
                fc, hi = pend.pop(0)
                for j in range(2):
                    for half in range(2):
                        bank = j * 2 + half
                        B.mm(ps[bank][:, :], st["hid"][hi][:, j * 128:(j + 1) * 128],
                             W2[:, fc, half * 512:(half + 1) * 512], start=(fc == 0), stop=(fc == 31),
                             R=["hid%d" % hi, "w2C2_%d" % (fc // 2)], W=[RP[bank]])

            for fc in range(32):
                bank = 4 + cnt["h"] % 3
                ri = cnt["h"] % 3
                hi = cnt["h"] % 6
                cnt["h"] += 1
                for kc in range(8):
                    B.mm(ps[bank][:, 0:256], W1[:, kc, fc * 128:(fc + 1) * 128], st["h2"][sl][:, kc, :],
                         start=(kc == 0), stop=(kc == 7), R=["w1C2_%d" % (fc // 16), hn], W=[RP[bank]])
                B.act(st["rl"][ri], ps[bank][:, 0:256], AF.Relu, R=[], W=[RP[bank], "rl%d" % ri])
                B.pool(_tt(st["hid"][hi], st["rl"][ri], st["rl"][ri], ALU.mult), R=["rl%d" % ri], W=["hid%d" % hi])
                pend.append((fc, hi))
                if len(pend) > 4:
                    l2()
            while pend:
                l2()
            xrs = []
            for j in range(2):
                xr = cnt["y"] % 2
                cnt["y"] += 1
                xrs.append(xr)
                x3 = st["x3"][xr]
                for half in range(2):
                    bank = j * 2 + half
                    B.dve(_tt(x3[:, half * 512:(half + 1) * 512], ps[bank],
                              st["x2"][sl][:, j, half * 512:(half + 1) * 512], ALU.add), R=[nm],
                          W=[RP[bank], "x3%d" % xr])
            for j in range(2):
                xr = xrs[j]
                x3 = st["x3"][xr]
                B.act(st["junk"], x3, AF.Square, R=["x3%d" % xr], W=["junk", "ssd%d" % xr], scale=1.0 / 32.0,
                      accum_out=st["ss"][:, xr:xr + 1])
                B.act(st["rstd"][:, xr:xr + 1], st["ss"][:, xr:xr + 1], AF.Ln, R=["ssd%d" % xr], W=["rsd%d" % xr],
                      bias=NORM_EPS, scale=1.0)
                B.act(st["rstd"][:, xr:xr + 1], st["rstd"][:, xr:xr + 1], AF.Exp, R=["rsd%d" % xr], W=["rstdd%d" % xr],
                      scale=-0.5)
                B.dve(_stt(st["yo"][xr], x3, st["rstd"][:, xr:xr + 1], st["fng"], ALU.mult, ALU.mult),
                      R=["x3%d" % xr, "rstdd%d" % xr, "fng"], W=["yo%d" % xr])
                B.dma(ys[i][b * 256 + j * 128: b * 256 + (j + 1) * 128, :], st["yo"][xr], "yo%d" % xr,
                      R=["yo%d" % xr], W=[])

        load(0)
        for b in range(nblk):
            if b + 1 < nblk:
                load(b + 1)
            block(b)

    C1 = {}
    C2 = {}
    phaseA()
    for i in range(nseq):
        S.barrier()
        phaseB1(i)
    for i in range(nseq):
        S.barrier()
        phaseB2(i)
    S.barrier()
    for i in range(nseq):
        phaseC1(i, i == 0)
    S.barrier()
    for i in range(nseq):
        phaseC2(i, i == 0)
    S.emit(nc, es)
    es.close()
    return sc


def make_consts(TMAX):
    bf = ml_dtypes.bfloat16
    ident = np.eye(128, dtype=np.float32).astype(bf)
    ones = np.ones((128, 128), dtype=np.float32).astype(bf)
    R = np.zeros((128, 128), dtype=np.float32)
    for c in range(2):
        for j in range(8):
            R[c * 64 + j, c * 64 + j + 8] = -1.0
            R[c * 64 + 8 + j, c * 64 + j] = 1.0
    rt = np.ascontiguousarray(R.T).astype(bf)
    pos = np.arange(TMAX, dtype=np.float32)
    inv_freq = (np.float32(ROPE_THETA) ** (-np.arange(0, 16, 2, dtype=np.float32) / np.float32(16))).astype(np.float32)
    ang = (pos[None, :] * inv_freq[:, None]).astype(np.float32)
    cos = np.ones((128, TMAX), dtype=np.float32)
    sin = np.zeros((128, TMAX), dtype=np.float32)
    for c in range(2):
        for j in range(8):
            cos[c * 64 + j] = np.cos(ang[j])
            cos[c * 64 + 8 + j] = np.cos(ang[j])
            sin[c * 64 + j] = np.sin(ang[j])
            sin[c * 64 + 8 + j] = np.sin(ang[j])
    s = np.arange(64)[:, None]
    t = np.arange(64)[None, :]
    gm = np.stack([np.tile((t >= s).astype(np.float32), (1, 4)), np.tile((t <= s).astype(np.float32), (1, 4))])
    reset = np.zeros((128, 2, 512), dtype=np.float32)
    reset[:, 0, ::64] = 1.0
    reset[:, 1, 63::64] = 1.0
    return dict(c_ident=ident, c_rt=rt, c_ones=ones, c_inv32=np.full((64, 128), 1.0 / 32.0, dtype=np.float32), c_cos=cos, c_sin=sin, c_gmask=gm.astype(np.float32),
                c_reset=reset)


_CACHE = {}


def run(seq_inputs, weights, debug=False, core_ids=None):
    TS = tuple(int(a.shape[0]) for a in seq_inputs[0])
    nc = bass.Bass("TRN2", target_bir_lowering=False)
    build_program(nc, TS, debug)
    consts = make_consts(max(TS))
    f = np.float32
    common = {
        "norm1": weights["norm1"].reshape(D).astype(f),
        "w_in": np.ascontiguousarray(weights["w_in"].reshape(D, INW)).astype(f),
        "hg_lb_logits": weights["hg_lb_logits"].astype(f),
        "hg_norm": weights["hg_norm"].reshape(512).astype(f),
        "w_hg_branch": weights["w_hg_branch"].reshape(512, D).astype(f),
        "lq1": weights["da_lambda_q1"].reshape(64).astype(f),
        "lk1": weights["da_lambda_k1"].reshape(64).astype(f),
        "lq2": weights["da_lambda_q2"].reshape(64).astype(f),
        "lk2": weights["da_lambda_k2"].reshape(64).astype(f),
        "da_subln": weights["da_subln"].reshape(128).astype(f),
        "w_da_branch": weights["w_da_branch"].reshape(512, D).astype(f),
        "w_out": weights["w_out"].reshape(D, D).astype(f),
        "norm2": weights["norm2"].reshape(D).astype(f),
        "w_mlp_in": weights["w_mlp_in"].reshape(D, DFF).astype(f),
        "w_mlp_out": weights["w_mlp_out"].reshape(DFF, D).astype(f),
        "final_norm": weights["final_norm"].reshape(D).astype(f),
    }
    common.update(consts)
    in_maps = []
    for core in seq_inputs:
        m = dict(common)
        for i, a in enumerate(core):
            m["x%d" % i] = np.ascontiguousarray(a, dtype=f)
        in_maps.append(m)
    if core_ids is None:
        core_ids = list(range(len(in_maps)))
    res = run_bass_kernel_spmd(nc, in_maps, core_ids=core_ids)
    return res


def kernel(x_prompt, x_sample, norm1, w_in, hg_lb_logits, hg_norm, w_hg_branch,
           da_lambda_q1, da_lambda_k1, da_lambda_q2, da_lambda_k2, da_subln, w_da_branch,
           w_out, norm2, w_mlp_in, w_mlp_out, final_norm):
    weights = dict(norm1=np.asarray(norm1), w_in=np.asarray(w_in), hg_lb_logits=np.asarray(hg_lb_logits),
                   hg_norm=np.asarray(hg_norm), w_hg_branch=np.asarray(w_hg_branch),
                   da_lambda_q1=np.asarray(da_lambda_q1), da_lambda_k1=np.asarray(da_lambda_k1),
                   da_lambda_q2=np.asarray(da_lambda_q2), da_lambda_k2=np.asarray(da_lambda_k2),
                   da_subln=np.asarray(da_subln), w_da_branch=np.asarray(w_da_branch), w_out=np.asarray(w_out),
                   norm2=np.asarray(norm2), w_mlp_in=np.asarray(w_mlp_in), w_mlp_out=np.asarray(w_mlp_out),
                   final_norm=np.asarray(final_norm))
    xp = np.asarray(x_prompt)
    xsm = np.asarray(x_sample)
    seq_inputs = [[xp[c], xsm[c]] for c in range(NCORES)]
    res = run(seq_inputs, weights)
    yp = np.stack([res.results[c]["y0"] for c in range(NCORES)]).astype(np.float32)
    ysm = np.stack([res.results[c]["y1"] for c in range(NCORES)]).astype(np.float32)
    return (yp, ysm)
```
